# Optimizing a Trainium2 kernel written in Bass

```python
import math
import jax, jax.numpy as jnp
from jax import lax
import numpy as np

D_MODEL = 2048
BATCH = 2
SEQ = 16384
DEPTH = 2
DEC_BATCH = 4
DEC_SEQ = 4096
PAST_LEN = 128

GRID_W = 64
PLE_DIM = 256
W_A = 512
S5_H = 16
S5_G = W_A // S5_H
S5_P = 64
W_B = 512
CONV_K = 3
N_HEADS = 8
HEAD_DIM = 64
W_C = N_HEADS * HEAD_DIM
NA_ROWS = 8
NA_COLS = 16
QCB = 16
KCB = 32
W_D = 512
FNET_GROUPS = 4
FNET_GW = W_D // FNET_GROUPS
N_BRANCH = 4
MIX_IN = W_A + 3 * W_B + 3 * W_C + W_D
IN_COLS = MIX_IN + N_BRANCH * D_MODEL
D_FF = -(-8 * D_MODEL // (3 * 256)) * 256
EPS = 1e-6
NEG_INF = -1e30

kernel_name = 'hybrid_bidir_s5_conv_natten_fnet_encoder'


def rmsnorm(x, g):
    xf = x.astype(jnp.float32)
    y = xf * lax.rsqrt(jnp.mean(xf * xf, axis=-1, keepdims=True) + EPS)
    return (y * g.astype(jnp.float32)).astype(x.dtype)


def cmul(ar, ai, br, bi):
    return ar * br - ai * bi, ar * bi + ai * br


def s5_scan(uf, lam_re, lam_im, log_dt, b_re, b_im, c_re, c_im, reverse):
    lam_re = lam_re.astype(jnp.float32)
    lam_im = lam_im.astype(jnp.float32)
    dt = jnp.exp(log_dt.astype(jnp.float32))[:, None]
    mag = jnp.exp(lam_re * dt)
    ar = mag * jnp.cos(lam_im * dt)
    ai = mag * jnp.sin(lam_im * dt)
    den = lam_re * lam_re + lam_im * lam_im
    fr = ((ar - 1.0) * lam_re + ai * lam_im) / den
    fi = (ai * lam_re - (ar - 1.0) * lam_im) / den
    bbr, bbi = cmul(fr[..., None], fi[..., None], b_re.astype(jnp.float32), b_im.astype(jnp.float32))
    bur = jnp.einsum('blgh,gph->blgp', uf, bbr)
    bui = jnp.einsum('blgh,gph->blgp', uf, bbi)
    a_r = jnp.broadcast_to(ar, bur.shape)
    a_i = jnp.broadcast_to(ai, bur.shape)

    def combine(e1, e2):
        a1r, a1i, b1r, b1i = e1
        a2r, a2i, b2r, b2i = e2
        nar, nai = cmul(a2r, a2i, a1r, a1i)
        tr, ti = cmul(a2r, a2i, b1r, b1i)
        return nar, nai, tr + b2r, ti + b2i

    _, _, xr, xi = lax.associative_scan(combine, (a_r, a_i, bur, bui), reverse=reverse, axis=1)
    return (jnp.einsum('blgp,ghp->blgh', xr, c_re.astype(jnp.float32))
            - jnp.einsum('blgp,ghp->blgh', xi, c_im.astype(jnp.float32)))


def s5_mixer(u, lam_re, lam_im, log_dt, b_re, b_im, c_re, c_im, d_skip, w_glu):
    bn, L, _ = u.shape
    uf = u.astype(jnp.float32).reshape(bn, L, S5_G, S5_H)
    y = d_skip.astype(jnp.float32).reshape(S5_G, S5_H) * uf
    for direction, rev in ((0, False), (1, True)):
        y = y + s5_scan(uf, lam_re[direction], lam_im[direction], log_dt[direction],
                        b_re[direction], b_im[direction], c_re[direction], c_im[direction], rev)
    y = jax.nn.gelu(y.reshape(bn, L, W_A))
    y = y * jax.nn.sigmoid(y @ w_glu.astype(jnp.float32))
    return y.astype(u.dtype)


def short_conv_mixer(bg, cg, v, conv_w):
    z = cg * v
    y = lax.conv_general_dilated(z, conv_w[:, None, :].astype(z.dtype), window_strides=(1,),
                                 padding=((CONV_K // 2, CONV_K // 2),),
                                 dimension_numbers=('NWC', 'WIO', 'NWC'),
                                 feature_group_count=W_B)
    return bg * y


def neighbourhood_attention(q, k, v, rel_bias):
    bn, L = q.shape[0], q.shape[1]
    rows = L // GRID_W
    wr = min(NA_ROWS, rows)
    n_cb = GRID_W // QCB
    r = np.arange(rows)
    rs = np.clip(r - wr // 2, 0, rows - wr)
    row_idx = rs[:, None] + np.arange(wr)[None, :]
    dr_idx = row_idx - r[:, None] + (NA_ROWS - 1)
    j = np.arange(n_cb)
    kc0 = np.clip(j * QCB - NA_COLS // 2, 0, GRID_W - KCB)
    col_idx = kc0[:, None] + np.arange(KCB)[None, :]
    qc = j[:, None] * QCB + np.arange(QCB)[None, :]
    cs = np.clip(qc - NA_COLS // 2, 0, GRID_W - NA_COLS)
    kcol = col_idx[:, None, :]
    valid = (kcol >= cs[:, :, None]) & (kcol < cs[:, :, None] + NA_COLS)
    dc_idx = np.clip(kcol - qc[:, :, None] + (NA_COLS - 1), 0, 2 * NA_COLS - 2)

    qg = q.reshape(bn, rows, n_cb, QCB, N_HEADS, HEAD_DIM)
    kg = k.reshape(bn, rows, GRID_W, N_HEADS, HEAD_DIM)
    vg = v.reshape(bn, rows, GRID_W, N_HEADS, HEAD_DIM)
    gi_r = row_idx[:, None, :, None]
    gi_c = col_idx[None, :, None, :]
    kn = kg[:, gi_r, gi_c]
    vn = vg[:, gi_r, gi_c]
    s = jnp.einsum('brjqhd,brjakhd->brjhqak', qg, kn,
                   preferred_element_type=jnp.float32) * (1.0 / math.sqrt(HEAD_DIM))
    bias = rel_bias.astype(jnp.float32)[:, dr_idx[:, None, None, :, None], dc_idx[None, :, :, None, :]]
    bias = jnp.transpose(bias, (1, 2, 0, 3, 4, 5))
    s = jnp.where(valid[None, :, None, :, None, :], s + bias, NEG_INF)
    pr = jax.nn.softmax(s.reshape(bn, rows, n_cb, N_HEADS, QCB, wr * KCB), axis=-1)
    pr = pr.reshape(bn, rows, n_cb, N_HEADS, QCB, wr, KCB).astype(v.dtype)
    o = jnp.einsum('brjhqak,brjakhd->brjqhd', pr, vn)
    return o.reshape(bn, L, W_C)


def fourier_mixer(u):
    bn, L, _ = u.shape
    ug = u.astype(jnp.float32).reshape(bn, L, FNET_GROUPS, FNET_GW)
    f = jnp.fft.fftn(ug, axes=(1, 3), norm='ortho').real
    return f.reshape(bn, L, W_D).astype(u.dtype)


def encoder_layer(x, p, g_mix, w_in, s5_lam_re, s5_lam_im, s5_log_dt, s5_b_re, s5_b_im,
                  s5_c_re, s5_c_im, s5_d, w_glu, conv_w, q_gain, k_gain, rel_bias,
                  w_br, w_o, g_ffn, w_ffn_in, w_ffn_out, g_ple, w_ple_gate, w_ple_proj):
    bn, L, _ = x.shape
    h = rmsnorm(x, g_mix)
    z = h @ w_in[:, :MIX_IN]
    offs = [W_A, W_A + W_B, W_A + 2 * W_B, W_A + 3 * W_B,
            W_A + 3 * W_B + W_C, W_A + 3 * W_B + 2 * W_C, W_A + 3 * W_B + 3 * W_C]
    u_a, b_g, c_g, v_b, q, k, v_c, u_d = jnp.split(z, offs, axis=-1)
    y_a = s5_mixer(u_a, s5_lam_re, s5_lam_im, s5_log_dt, s5_b_re, s5_b_im, s5_c_re, s5_c_im, s5_d, w_glu)
    y_b = short_conv_mixer(b_g, c_g, v_b, conv_w)
    qh = rmsnorm(q.reshape(bn, L, N_HEADS, HEAD_DIM), q_gain)
    kh = rmsnorm(k.reshape(bn, L, N_HEADS, HEAD_DIM), k_gain)
    vh = v_c.reshape(bn, L, N_HEADS, HEAD_DIM)
    y_c = neighbourhood_attention(qh, kh, vh, rel_bias)
    y_d = fourier_mixer(u_d)
    ys = (y_a, y_b, y_c, y_d)
    merged = None
    for kb in range(N_BRANCH):
        gate = jax.nn.sigmoid(h @ w_in[:, MIX_IN + kb * D_MODEL: MIX_IN + (kb + 1) * D_MODEL])
        term = gate * (ys[kb] @ w_br[kb])
        merged = term if merged is None else merged + term
    x = x + merged @ w_o
    h2 = rmsnorm(x, g_ffn)
    a, b = jnp.split(h2 @ w_ffn_in, 2, axis=-1)
    x = x + (jax.nn.silu(a) * b) @ w_ffn_out
    pg = jax.nn.sigmoid(rmsnorm(x, g_ple) @ w_ple_gate)
    x = x + pg * (p @ w_ple_proj)
    return x


def setup_inputs(seed: int = 0) -> dict:
    key = jax.random.key(seed)
    ks = iter(jax.random.split(key, 40))
    f32 = jnp.float32

    def nrm(shape, scale):
        return jax.random.normal(next(ks), shape, f32) * scale

    x_prompt = nrm((BATCH, SEQ, D_MODEL), 1.0)
    x_sample = nrm((DEC_BATCH, DEC_SEQ, D_MODEL), 1.0)
    p_prompt = nrm((DEPTH, BATCH, SEQ, PLE_DIM), 1.0)
    p_sample = nrm((DEPTH, DEC_BATCH, DEC_SEQ, PLE_DIM), 1.0)
    g_mix = 1.0 + nrm((DEPTH, D_MODEL), 0.02)
    w_in = nrm((DEPTH, D_MODEL, IN_COLS), D_MODEL ** -0.5)
    n_idx = jnp.arange(S5_P, dtype=f32)
    s5_lam_re = -0.5 + nrm((DEPTH, 2, S5_G, S5_P), 0.01)
    s5_lam_im = math.pi * n_idx + nrm((DEPTH, 2, S5_G, S5_P), 0.01)
    s5_log_dt = jax.random.uniform(next(ks), (DEPTH, 2, S5_G), f32, math.log(1e-3), math.log(1e-1))
    s5_b_re = nrm((DEPTH, 2, S5_G, S5_P, S5_H), (2 * S5_H) ** -0.5)
    s5_b_im = nrm((DEPTH, 2, S5_G, S5_P, S5_H), (2 * S5_H) ** -0.5)
    s5_c_re = nrm((DEPTH, 2, S5_G, S5_H, S5_P), (2 * S5_P) ** -0.5)
    s5_c_im = nrm((DEPTH, 2, S5_G, S5_H, S5_P), (2 * S5_P) ** -0.5)
    s5_d = nrm((DEPTH, W_A), 1.0)
    w_glu = nrm((DEPTH, W_A, W_A), W_A ** -0.5)
    conv_w = nrm((DEPTH, CONV_K, W_B), CONV_K ** -0.5)
    q_gain = 1.0 + nrm((DEPTH, HEAD_DIM), 0.02)
    k_gain = 1.0 + nrm((DEPTH, HEAD_DIM), 0.02)
    rel_bias = nrm((DEPTH, N_HEADS, 2 * NA_ROWS - 1, 2 * NA_COLS - 1), 0.02)
    w_br = nrm((DEPTH, N_BRANCH, W_A, D_MODEL), W_A ** -0.5)
    w_o = nrm((DEPTH, D_MODEL, D_MODEL), D_MODEL ** -0.5)
    g_ffn = 1.0 + nrm((DEPTH, D_MODEL), 0.02)
    w_ffn_in = nrm((DEPTH, D_MODEL, 2 * D_FF), D_MODEL ** -0.5)
    w_ffn_out = nrm((DEPTH, D_FF, D_MODEL), D_FF ** -0.5)
    g_ple = 1.0 + nrm((DEPTH, D_MODEL), 0.02)
    w_ple_gate = nrm((DEPTH, D_MODEL, D_MODEL), D_MODEL ** -0.5)
    w_ple_proj = nrm((DEPTH, PLE_DIM, D_MODEL), PLE_DIM ** -0.5)
    return {'x_prompt': x_prompt, 'x_sample': x_sample, 'p_prompt': p_prompt, 'p_sample': p_sample,
            'g_mix': g_mix, 'w_in': w_in, 's5_lam_re': s5_lam_re, 's5_lam_im': s5_lam_im,
            's5_log_dt': s5_log_dt, 's5_b_re': s5_b_re, 's5_b_im': s5_b_im, 's5_c_re': s5_c_re,
            's5_c_im': s5_c_im, 's5_d': s5_d, 'w_glu': w_glu, 'conv_w': conv_w, 'q_gain': q_gain,
            'k_gain': k_gain, 'rel_bias': rel_bias, 'w_br': w_br, 'w_o': w_o, 'g_ffn': g_ffn,
            'w_ffn_in': w_ffn_in, 'w_ffn_out': w_ffn_out, 'g_ple': g_ple, 'w_ple_gate': w_ple_gate,
            'w_ple_proj': w_ple_proj}


def reference(x_prompt, x_sample, p_prompt, p_sample, g_mix, w_in, s5_lam_re, s5_lam_im,
              s5_log_dt, s5_b_re, s5_b_im, s5_c_re, s5_c_im, s5_d, w_glu, conv_w, q_gain,
              k_gain, rel_bias, w_br, w_o, g_ffn, w_ffn_in, w_ffn_out, g_ple, w_ple_gate,
              w_ple_proj):
    def trunk(x, p):
        for i in range(DEPTH):
            x = encoder_layer(x, p[i], g_mix[i], w_in[i], s5_lam_re[i], s5_lam_im[i], s5_log_dt[i],
                              s5_b_re[i], s5_b_im[i], s5_c_re[i], s5_c_im[i], s5_d[i], w_glu[i],
                              conv_w[i], q_gain[i], k_gain[i], rel_bias[i], w_br[i], w_o[i],
                              g_ffn[i], w_ffn_in[i], w_ffn_out[i], g_ple[i], w_ple_gate[i],
                              w_ple_proj[i])
        return x

    y_prompt = trunk(x_prompt, p_prompt)
    y_sample = trunk(x_sample, p_sample)
    return (y_prompt, y_sample)
```

```python
import math
from contextlib import ExitStack
import numpy as np
import ml_dtypes
import concourse.bass as bass
import concourse.mybir as mybir
from concourse.bass_utils import run_bass_kernel_spmd

F32 = mybir.dt.float32
BF16 = mybir.dt.bfloat16
AF = mybir.ActivationFunctionType
ALU = mybir.AluOpType
AX = mybir.AxisListType

D = 2048
DEPTH = 2
PLE = 256
DFF = 5632
MIX = 4096
ZC = 3584 + 1024
EPS = 1e-6
TT = 512
DBG0 = 0
NLAYER = DEPTH


DEBUG_Y = False
MIX_ENABLE = dict(conv=True, fnet=True, s5=True, na=True)


def MIXERS(kb, nc, sb, sname, L, l, zd, zdb, yd, ydb, locals_):
    lc = locals_
    P = lc["P"]
    pbank, pbuf = lc["pbank"], lc["pbuf"]
    prr = [0]

    def bank():
        i = prr[0] % 6
        prr[0] += 1
        return pbank[i], pbuf[i]

    def zero_cols(c0, c1):
        lc["new_phase"]("zero")
        t = sb("mixz", [128, 512], BF16)
        b = kb.buf("mixz")
        kb.op("pool", lambda eng: eng.memset(t[:], 0.0), writes=[b])
        for r in range(L // 128):
            kb.dma("pool", yd[r * 128:(r + 1) * 128, c0:c1], t[:, 0:c1 - c0], src=b, dst=ydb)
        lc["end_phase"]()

    def conv():
        lc["new_phase"]("conv" + sname)
        n = L // 128
        nb = min(n, 16)
        zd3 = zd.rearrange("(p n) c -> p n c", n=n)
        yd3 = yd.rearrange("(p n) c -> p n c", n=n)
        CV = sb("cv", [128, nb + 2, 1024], BF16); CVb = kb.buf("cv")
        BG = sb("bg", [128, nb, 512], BF16); BGb = kb.buf("bg")
        Z = sb("cz", [128, nb + 2, 512], F32); Zb = kb.buf("cz")
        ACC = sb("cacc", [128, nb, 512], F32); ACCb = kb.buf("cacc")
        T1 = sb("ct1", [128, nb, 512], F32); T1b = kb.buf("ct1")
        YB = sb("cyb", [128, nb, 512], BF16); YBb = kb.buf("cyb")
        Wt = sb("cw", [128, 3, 512], F32); Wb = kb.buf("cw")
        kb.dma("sp", Wt[:], P["conv_w"][l:l + 1, :, :].to_broadcast([128, 3, 512]), dst=Wb)
        for j0 in range(0, n, nb):
            kb.dma("sp", CV[:, 1:nb + 1, :], zd3[:, j0:j0 + nb, 1024:2048], src=zdb, dst=CVb)
            if j0 > 0:
                kb.dma("sp", CV[:, 0:1, :], zd3[:, j0 - 1:j0, 1024:2048], src=zdb, dst=CVb)
            else:
                kb.op("pool", lambda eng: eng.memset(CV[:, 0:1, :], 0.0), writes=[CVb])
                kb.dma("sp", CV[1:128, 0:1, :], zd3[0:127, n - 1:n, 1024:2048], src=zdb, dst=CVb)
            if j0 + nb < n:
                kb.dma("sp", CV[:, nb + 1:nb + 2, :], zd3[:, j0 + nb:j0 + nb + 1, 1024:2048], src=zdb, dst=CVb)
            else:
                kb.op("pool", lambda eng: eng.memset(CV[:, nb + 1:nb + 2, :], 0.0), writes=[CVb])
                kb.dma("sp", CV[0:127, nb + 1:nb + 2, :], zd3[1:128, 0:1, 1024:2048], src=zdb, dst=CVb)
            kb.dma("sp", BG[:], zd3[:, j0:j0 + nb, 512:1024], src=zdb, dst=BGb)
            kb.op("dve", lambda eng: eng.tensor_tensor(out=Z[:], in0=CV[:, :, 0:512], in1=CV[:, :, 512:1024], op=ALU.mult),
                  reads=[CVb], writes=[Zb])

            def wb_(k):
                return Wt[:, k:k + 1, :].to_broadcast([128, nb, 512])
            kb.op("dve", lambda eng: eng.tensor_tensor(out=ACC[:], in0=Z[:, 0:nb, :], in1=wb_(0), op=ALU.mult),
                  reads=[Zb, Wb], writes=[ACCb])
            for k in (1, 2):
                kb.op("pool", lambda eng, k=k: eng.tensor_tensor(out=T1[:], in0=Z[:, k:k + nb, :], in1=wb_(k), op=ALU.mult),
                      reads=[Zb, Wb], writes=[T1b])
                kb.op("dve", lambda eng: eng.tensor_tensor(out=ACC[:], in0=ACC[:], in1=T1[:], op=ALU.add),
                      reads=[T1b, ACCb], writes=[ACCb])
            kb.op("dve", lambda eng: eng.tensor_tensor(out=YB[:], in0=ACC[:], in1=BG[:], op=ALU.mult),
                  reads=[ACCb, BGb], writes=[YBb])
            kb.dma("pool", yd3[:, j0:j0 + nb, 512:1024], YB[:], src=YBb, dst=ydb)
        lc["end_phase"]()

    def fnet():
        lc["new_phase"]("fnet" + sname)
        ft = lc["ft"]
        N2 = L // 128
        V = sb("fV", [128, N2, 256], BF16); Vb = kb.buf("fV")
        Y = sb("fY", [N2, 128, 128], BF16); Yb = kb.buf("fY")
        r1 = sb("fr1", [128, 256], BF16); r2 = sb("fr2", [128, 256], BF16); rb = kb.buf("fr")
        tcd = sb("ftc", [N2, 256], F32); tsd = sb("fts", [N2, 256], F32); tb_ = kb.buf("ft")
        c2 = sb("fc2", [N2, N2], BF16); s2n = sb("fs2", [N2, N2], BF16); cb_ = kb.buf("fc")
        P1 = [sb(f"fp1{i}", [N2, 2, 256], F32) for i in range(2)]; P1b = [kb.buf(f"fp1{i}") for i in range(2)]
        P2 = [sb(f"fp2{i}", [N2, 2, 256], F32) for i in range(2)]; P2b = [kb.buf(f"fp2{i}") for i in range(2)]
        APB = [sb(f"fab{i}", [N2, 4, 256], BF16) for i in range(2)]; APBb = [kb.buf(f"fab{i}") for i in range(2)]
        kb.dma("sp", r1[:], ft["r1"], dst=rb); kb.dma("sp", r2[:], ft["r2"], dst=rb)
        for h in range(2):
            kb.dma("sp", tcd[:, h * 128:(h + 1) * 128], ft["tc"], dst=tb_)
            kb.dma("sp", tsd[:, h * 128:(h + 1) * 128], ft["ts"], dst=tb_)
        kb.dma("sp", c2[:], ft["c2"], dst=cb_); kb.dma("sp", s2n[:], ft["s2n"], dst=cb_)
        zv = zd.rearrange("(a b) c -> a b c", b=N2)
        yv = yd.rearrange("(k2 k1) c -> k2 k1 c", k1=128)
        cnt = 0
        for g in range(4):
            for h0 in range(0, 128, 32):
                kb.dma("sp", V[h0:h0 + 32], zv[h0:h0 + 32, :, 3584 + g * 256: 3584 + (g + 1) * 256], src=zdb, dst=Vb)
            for j4 in range(0, 128, 4):
                ai = (j4 // 4) % 2
                for hp in range(2):
                    pi = cnt % 2
                    cnt += 1
                    A, Ab = bank()
                    for jj in range(2):
                        j = j4 + hp * 2 + jj
                        kb.op("pe", lambda eng, o=A[0:N2, jj * 256:(jj + 1) * 256], a=V[:, :, j]: eng.matmul(o, a, r1[:], start=True, stop=False),
                              reads=[Vb, rb], writes=[Ab])
                        kb.op("pe", lambda eng, o=A[0:N2, jj * 256:(jj + 1) * 256], a=V[:, :, 128 + j]: eng.matmul(o, a, r2[:], start=False, stop=True),
                              reads=[Vb, rb], writes=[Ab])
                    Av = A[0:N2, :].rearrange("p (j c) -> p j c", c=256)
                    kb.op("dve", lambda eng, o=P1[pi][:], a=Av: eng.tensor_tensor(out=o, in0=a, in1=tcd[:].unsqueeze(1).to_broadcast([N2, 2, 256]), op=ALU.mult),
                          reads=[Ab, tb_], writes=[P1b[pi]])
                    kb.op("dve", lambda eng, o=P2[pi][:], a=Av: eng.tensor_tensor(out=o, in0=a, in1=tsd[:].unsqueeze(1).to_broadcast([N2, 2, 256]), op=ALU.mult),
                          reads=[Ab, tb_], writes=[P2b[pi]])
                    kb.op("pool", lambda eng, o=APB[ai][:, hp * 2:hp * 2 + 2, 0:128], a=P1[pi][:, :, 0:128], b=P2[pi][:, :, 128:256]:
                          eng.tensor_tensor(out=o, in0=a, in1=b, op=ALU.subtract), reads=[P1b[pi], P2b[pi]], writes=[APBb[ai]])
                    kb.op("pool", lambda eng, o=APB[ai][:, hp * 2:hp * 2 + 2, 128:256], a=P1[pi][:, :, 128:256], b=P2[pi][:, :, 0:128]:
                          eng.tensor_tensor(out=o, in0=a, in1=b, op=ALU.add), reads=[P1b[pi], P2b[pi]], writes=[APBb[ai]])
                Fb, Fbb = bank()
                for c in range(4):
                    kb.op("pe", lambda eng, o=Fb[0:N2, c * 128:(c + 1) * 128], b=APB[ai][:, c, 0:128]: eng.matmul(o, c2[:], b, start=True, stop=False),
                          reads=[APBb[ai], cb_], writes=[Fbb])
                    kb.op("pe", lambda eng, o=Fb[0:N2, c * 128:(c + 1) * 128], b=APB[ai][:, c, 128:256]: eng.matmul(o, s2n[:], b, start=False, stop=True),
                          reads=[APBb[ai], cb_], writes=[Fbb])
                kb.op("act", lambda eng, o=Y[:, :, j4:j4 + 4].rearrange("p k c -> p c k"), a=Fb[0:N2, :].rearrange("p (c k) -> p c k", k=128):
                      eng.copy(out=o, in_=a), reads=[Fbb], writes=[Yb])
            st_ = max(1, N2 // 4)
            for h0 in range(0, N2, st_):
                kb.dma("pool", yv[h0:h0 + st_, :, 1536 + g * 128: 1536 + (g + 1) * 128], Y[h0:h0 + st_], src=Yb, dst=ydb)
        lc["end_phase"]()


    def s5():
        lc["new_phase"]("s5" + sname)
        C = lc["CST"]
        ident, identb = lc["ident"], lc["identb"]
        tbank, tbuf = lc["tbank"], lc["tbuf"]
        T = min(512, L)
        nch = L // T
        nsub = T // 128
        nblk = L // 128
        TWO_PI = 2.0 * math.pi
        iota1 = sb("s_iota", [128, 512], F32); maskh = sb("s_mh", [128, 4], F32); maskp = sb("s_mp", [128, 2], F32)
        maskpn = sb("s_mpn", [128, 2], F32); Jt = sb("s_J", [128, 128], BF16); pit = sb("s_pi", [128, 1], F32)
        cb = kb.buf("s_const")
        for t_, src in ((iota1, C["iota1"]), (maskh, C["maskh"]), (maskp, C["maskp"]), (maskpn, C["maskpn"]), (Jt, C["J"]), (pit, C["pi"])):
            kb.dma("sp", t_[:], src, dst=cb)
        Dt = sb("s_D", [128, 512], F32); Db = kb.buf("s_D")
        kb.dma("sp", Dt[:], P["s5_d"][l:l + 1, :].to_broadcast([128, 512]), dst=Db)
        Bbd = {}; Cbd = {}; RHO = {}; THM = {}
        pre = {}
        for d in range(2):
            pre["rho%d" % d] = sb("s_rho%d" % d, [128, 16], F32)
            pre["s1%d" % d] = sb("s_s1%d" % d, [128, 16], F32)
            pre["c1%d" % d] = sb("s_c1%d" % d, [128, 16], F32)
            for ri in range(2):
                pre["bbd%d%d" % (d, ri)] = sb("s_bbd%d%d" % (d, ri), [128, 4, 2, 2, 64], BF16)
                pre["cbd%d%d" % (d, ri)] = sb("s_cbd%d%d" % (d, ri), [128, 16, 2, 16], F32)
        pre["hpi"] = sb("s_hpi", [128, 1], F32)
        lc["push_scope"]()
        tmpn = [0]

        def tmp(shape):
            tmpn[0] += 1
            return sb("s_tmp%d" % tmpn[0], shape, F32), kb.buf("s_tmp%d" % tmpn[0])

        def tt(eng, out, a, b, op, reads, writes):
            kb.op(eng, lambda e, out=out, a=a, b=b, op=op: e.tensor_tensor(out=out, in0=a, in1=b, op=op), reads=reads, writes=writes)

        hpit = pre["hpi"]
        kb.op("pool", lambda e: e.memset(hpit[:], 0.5 * math.pi), writes=[cb])

        def sincos(src, srcb, shape, sn=None, snb=None, cs=None, csb=None):
            if sn is None:
                sn, snb = tmp(shape); cs, csb = tmp(shape)
            ta, tab_ = tmp(shape); tb2, tb2b = tmp(shape)
            kb.op("act", lambda e: e.activation(out=sn[:], in_=src, func=AF.Sin, scale=1.0 / 16.0), reads=[srcb], writes=[snb])
            kb.op("act", lambda e: e.activation(out=cs[:], in_=src, func=AF.Sin, bias=hpit[:], scale=-1.0 / 16.0), reads=[srcb, cb], writes=[csb])
            for _ in range(4):
                tt("dve", ta[:], cs[:], cs[:], ALU.mult, [csb], [tab_])
                tt("dve", tb2[:], sn[:], sn[:], ALU.mult, [snb], [tb2b])
                tt("dve", sn[:], sn[:], cs[:], ALU.mult, [snb, csb], [snb])
                kb.op("dve", lambda e: e.tensor_scalar(out=sn[:], in0=sn[:], scalar1=2.0, scalar2=None, op0=ALU.mult), reads=[snb], writes=[snb])
                tt("dve", cs[:], ta[:], tb2[:], ALU.subtract, [tab_, tb2b], [csb])
            return sn, snb, cs, csb

        for d in range(2):
            tmpn[0] = 100 * d
            LRs, LRsb = tmp([128, 16]); LIs, LIsb = tmp([128, 16]); LDs, LDsb = tmp([128, 16])
            kb.dma("sp", LRs[:], P["lam_re"][l, d].rearrange("(q g2) p -> (g2 p) q", g2=2), dst=LRsb)
            kb.dma("sp", LIs[:], P["lam_im"][l, d].rearrange("(q g2) p -> (g2 p) q", g2=2), dst=LIsb)
            ldv = P["log_dt"][l, d].rearrange("(q g2) -> g2 q", g2=2)
            for g2 in range(2):
                kb.dma("sp", LDs[g2 * 64:(g2 + 1) * 64, :], ldv[g2:g2 + 1, :].to_broadcast([64, 16]), dst=LDsb)
            DTs, DTsb = tmp([128, 16])
            kb.op("act", lambda e, o=DTs, a=LDs: e.activation(out=o[:], in_=a[:], func=AF.Exp), reads=[LDsb], writes=[DTsb])
            t0_, t0b = tmp([128, 16])
            tt("dve", t0_[:], LRs[:], DTs[:], ALU.mult, [LRsb, DTsb], [t0b])
            rho = pre["rho%d" % d]; rhob = kb.buf("s_rho")
            kb.op("act", lambda e, o=rho, a=t0_: e.activation(out=o[:], in_=a[:], func=AF.Exp), reads=[t0b], writes=[rhob])
            th, thb = tmp([128, 16])
            tt("dve", th[:], LIs[:], DTs[:], ALU.mult, [LIsb, DTsb], [thb])
            s1b = kb.buf("s_s1"); c1b = kb.buf("s_c1")
            sincos(th[:], thb, [128, 16], pre["s1%d" % d], s1b, pre["c1%d" % d], c1b)
            RHO[d] = (rho, rhob); THM[d] = (pre["s1%d" % d], s1b, pre["c1%d" % d], c1b)
            LRc, LRcb = tmp([128, 4, 64]); LIc, LIcb = tmp([128, 4, 64]); LDc, LDcb = tmp([128, 4])
            BRc, BRcb = tmp([128, 4, 64]); BIc, BIcb = tmp([128, 4, 64])
            lrv = P["lam_re"][l, d].rearrange("(ct pg) p -> pg ct p", pg=8)
            liv = P["lam_im"][l, d].rearrange("(ct pg) p -> pg ct p", pg=8)
            ldv2 = P["log_dt"][l, d].rearrange("(ct pg) -> pg ct", pg=8)
            brv = P["b_re"][l, d].rearrange("(ct pg) p h -> pg h ct p", pg=8)
            biv = P["b_im"][l, d].rearrange("(ct pg) p h -> pg h ct p", pg=8)
            for pg in range(8):
                sl = slice(pg * 16, (pg + 1) * 16)
                kb.dma("sp", LRc[sl], lrv[pg:pg + 1].to_broadcast([16, 4, 64]), dst=LRcb)
                kb.dma("sp", LIc[sl], liv[pg:pg + 1].to_broadcast([16, 4, 64]), dst=LIcb)
                kb.dma("sp", LDc[sl], ldv2[pg:pg + 1].to_broadcast([16, 4]), dst=LDcb)
                for ct_ in range(4):
                    kb.dma("sp", BRc[sl, ct_, :], brv[pg, :, ct_, :], dst=BRcb)
                    kb.dma("sp", BIc[sl, ct_, :], biv[pg, :, ct_, :], dst=BIcb)
            DTc, DTcb = tmp([128, 4])
            kb.op("act", lambda e, o=DTc, a=LDc: e.activation(out=o[:], in_=a[:], func=AF.Exp), reads=[LDcb], writes=[DTcb])
            dtb = DTc[:].unsqueeze(2).to_broadcast([128, 4, 64])
            lrdt, lrdtb = tmp([128, 4, 64])
            tt("dve", lrdt[:], LRc[:], dtb, ALU.mult, [LRcb, DTcb], [lrdtb])
            rhoc, rhocb = tmp([128, 4, 64])
            kb.op("act", lambda e, o=rhoc, a=lrdt: e.activation(out=o[:], in_=a[:], func=AF.Exp), reads=[lrdtb], writes=[rhocb])
            thc, thcb = tmp([128, 4, 64])
            tt("dve", thc[:], LIc[:], dtb, ALU.mult, [LIcb, DTcb], [thcb])
            sn, snb, cs, csb = sincos(thc[:], thcb, [128, 4, 64])
            ar, arb = tmp([128, 4, 64]); ai, aib = tmp([128, 4, 64])
            tt("dve", ar[:], rhoc[:], cs[:], ALU.mult, [rhocb, csb], [arb])
            tt("dve", ai[:], rhoc[:], sn[:], ALU.mult, [rhocb, snb], [aib])
            kb.op("dve", lambda e, a=ar: e.tensor_scalar(out=a[:], in0=a[:], scalar1=-1.0, scalar2=None, op0=ALU.add), reads=[arb], writes=[arb])
            den, denb = tmp([128, 4, 64]); t1, t1b = tmp([128, 4, 64])
            tt("dve", den[:], LRc[:], LRc[:], ALU.mult, [LRcb], [denb])
            tt("dve", t1[:], LIc[:], LIc[:], ALU.mult, [LIcb], [t1b])
            tt("dve", den[:], den[:], t1[:], ALU.add, [denb, t1b], [denb])
            kb.op("dve", lambda e, a=den: e.reciprocal(out=a[:], in_=a[:]), reads=[denb], writes=[denb])
            fr, frb = tmp([128, 4, 64]); fi, fib = tmp([128, 4, 64]); t2, t2b = tmp([128, 4, 64])
            tt("dve", fr[:], ar[:], LRc[:], ALU.mult, [arb, LRcb], [frb])
            tt("dve", t2[:], ai[:], LIc[:], ALU.mult, [aib, LIcb], [t2b])
            tt("dve", fr[:], fr[:], t2[:], ALU.add, [frb, t2b], [frb])
            tt("dve", fr[:], fr[:], den[:], ALU.mult, [frb, denb], [frb])
            tt("dve", fi[:], ai[:], LRc[:], ALU.mult, [aib, LRcb], [fib])
            tt("dve", t2[:], ar[:], LIc[:], ALU.mult, [arb, LIcb], [t2b])
            tt("dve", fi[:], fi[:], t2[:], ALU.subtract, [fib, t2b], [fib])
            tt("dve", fi[:], fi[:], den[:], ALU.mult, [fib, denb], [fib])
            bbr, bbrb = tmp([128, 4, 64]); bbi, bbib = tmp([128, 4, 64])
            tt("dve", bbr[:], fr[:], BRc[:], ALU.mult, [frb, BRcb], [bbrb])
            tt("dve", t2[:], fi[:], BIc[:], ALU.mult, [fib, BIcb], [t2b])
            tt("dve", bbr[:], bbr[:], t2[:], ALU.subtract, [bbrb, t2b], [bbrb])
            tt("dve", bbi[:], fr[:], BIc[:], ALU.mult, [frb, BIcb], [bbib])
            tt("dve", t2[:], fi[:], BRc[:], ALU.mult, [fib, BRcb], [t2b])
            tt("dve", bbi[:], bbi[:], t2[:], ALU.add, [bbib, t2b], [bbib])
            for ri, src_, srcb_ in ((0, bbr, bbrb), (1, bbi, bbib)):
                bd = pre["bbd%d%d" % (d, ri)]; bdb = kb.buf("s_bbd")
                for sg in range(4):
                    kb.op("dve", lambda e, o=bd[:, :, sg // 2, sg % 2, :], a=src_, sg=sg: e.tensor_scalar(out=o, in0=a[:], scalar1=maskh[:, sg:sg + 1], scalar2=None, op0=ALU.mult),
                          reads=[srcb_, cb], writes=[bdb])
                Bbd[(d, ri)] = (bd, bdb)
            for ri, key, mk in ((0, "c_re", maskp), (1, "c_im", maskpn)):
                cs_, csb_ = tmp([128, 16, 16])
                cv = P[key][l, d].rearrange("(q g2) ho p -> g2 p q ho", g2=2)
                for g2 in range(2):
                    for q_ in range(16):
                        kb.dma("sp", cs_[g2 * 64:(g2 + 1) * 64, q_, :], cv[g2, :, q_, :], dst=csb_)
                cd = pre["cbd%d%d" % (d, ri)]; cdb = kb.buf("s_cbd")
                for g2 in range(2):
                    kb.op("dve", lambda e, o=cd[:, :, g2, :], a=cs_, g2=g2, mk=mk: e.tensor_scalar(out=o, in0=a[:], scalar1=mk[:, g2:g2 + 1], scalar2=None, op0=ALU.mult),
                          reads=[csb_, cb], writes=[cdb])
                Cbd[(d, ri)] = (cd, cdb)
        lc["pop_scope"]()
        uT = sb("s_uT", [128, L], BF16); uTb = kb.buf("s_uT")
        uTr = sb("s_uTr", [128, L], BF16); uTrb = kb.buf("s_uTr")
        ub = [sb("s_ub%d" % i, [128, 4, 128], BF16) for i in range(2)]; ubb = [kb.buf("s_ub%d" % i) for i in range(2)]
        Yacc = [sb("s_Y%d" % d, [128, nblk, 128], BF16) for d in range(2)]
        Yaccb = [[kb.buf("s_Y%d_%d" % (d, c)) for c in range(nch)] for d in range(2)]
        wk = {nm: (sb("s_w_" + nm, [128, T], F32), kb.buf("s_w_" + nm)) for nm in
              ("t1", "t2", "t3", "t4", "kr", "ki", "wr", "wi", "p1", "p2", "p3", "p4", "xr", "xi")}
        sint = sb("s_sin", [128, T], F32); cost = sb("s_cos", [128, T], F32); m1t = sb("s_m1", [128, T], F32); m2t = sb("s_m2", [128, T], F32)
        sinb = kb.buf("s_sin"); cosb = kb.buf("s_cos"); m1b_ = kb.buf("s_m1"); m2b_ = kb.buf("s_m2")
        cry = sb("s_cry", [128, 2], F32); cryb = kb.buf("s_cry")
        fin = {nm: (sb("s_f_" + nm, [128, 512], F32), kb.buf("s_f_" + nm)) for nm in ("s2", "x2", "sg", "du")}
        yo = sb("s_yo", [128, 4, 128], BF16); yob = kb.buf("s_yo")
        nb4 = min(4, nblk)
        ui = 0
        for ct in range(4):
            for b0 in range(0, nblk, nb4):
                i = ui % 2
                ui += 1
                kb.dma("sp", ub[i][:, 0:nb4, :], zd[b0 * 128:(b0 + nb4) * 128, ct * 128:(ct + 1) * 128].rearrange("(b t) c -> t b c", t=128),
                       src=zdb, dst=ubb[i])
                for rev in range(2):
                    tbk, tbkb = tbank[rev], tbuf[rev]
                    for bb in range(nb4):
                        col = bb if rev == 0 else (nb4 - 1 - bb)
                        kb.op("pe", lambda e, o=tbk[:, col * 128:(col + 1) * 128], a=ub[i][:, bb, :], idm=(ident if rev == 0 else Jt):
                              e.transpose(out=o, in_=a, identity=idm[:]), reads=[ubb[i], identb, cb], writes=[tbkb])
                    if rev == 0:
                        kb.op("act", lambda e, o=uT[:, b0 * 128:(b0 + nb4) * 128], a=tbk[:, 0:nb4 * 128]: e.copy(out=o, in_=a), reads=[tbkb], writes=[uTb])
                    else:
                        r0 = L - (b0 + nb4) * 128
                        kb.op("act", lambda e, o=uTr[:, r0:r0 + nb4 * 128], a=tbk[:, 0:nb4 * 128]: e.copy(out=o, in_=a), reads=[tbkb], writes=[uTrb])
            for d in range(2):
                U, Ub = (uT, uTb) if d == 0 else (uTr, uTrb)
                rho, rhob = RHO[d]; s1, s1b, c1, c1b = THM[d]
                for q4 in range(4):
                    q = ct * 4 + q4
                    kb.op("act", lambda e, q=q, c1=c1: e.copy(out=cost[:, 0:1], in_=c1[:, q:q + 1]), reads=[c1b], writes=[cosb])
                    kb.op("act", lambda e, q=q, s1=s1: e.copy(out=sint[:, 0:1], in_=s1[:, q:q + 1]), reads=[s1b], writes=[sinb])
                    m_ = 1
                    while m_ < T:
                        kb.op("pool", lambda e, m_=m_: e.tensor_scalar(out=m1t[:, 0:m_], in0=sint[:, 0:m_], scalar1=sint[:, m_ - 1:m_], scalar2=None, op0=ALU.mult),
                              reads=[sinb], writes=[m1b_])
                        kb.op("pool", lambda e, m_=m_: e.tensor_scalar(out=m2t[:, 0:m_], in0=sint[:, 0:m_], scalar1=cost[:, m_ - 1:m_], scalar2=None, op0=ALU.mult),
                              reads=[sinb, cosb], writes=[m2b_])
                        kb.op("dve", lambda e, m_=m_: e.scalar_tensor_tensor(out=sint[:, m_:2 * m_], in0=cost[:, 0:m_], scalar=sint[:, m_ - 1:m_], in1=m2t[:, 0:m_],
                                                                           op0=ALU.mult, op1=ALU.add), reads=[cosb, sinb, m2b_], writes=[sinb])
                        kb.op("dve", lambda e, m_=m_: e.scalar_tensor_tensor(out=cost[:, m_:2 * m_], in0=cost[:, 0:m_], scalar=cost[:, m_ - 1:m_], in1=m1t[:, 0:m_],
                                                                           op0=ALU.mult, op1=ALU.subtract), reads=[cosb, m1b_], writes=[cosb])
                        m_ *= 2
                    kb.op("pool", lambda e: e.memset(cry[:], 0.0), writes=[cryb])
                    for c in range(nch):
                        pr, prb = bank(); pi_, pib = bank()
                        for ri, (pp_, ppb_) in ((0, (pr, prb)), (1, (pi_, pib))):
                            bd, bdb = Bbd[(d, ri)]
                            hs = slice((q4 // 2) * 64, (q4 // 2 + 1) * 64)
                            kb.op("pe", lambda e, o=pp_[:, 0:T], a=bd[hs, ct, q4 % 2, :, :].rearrange("p a b -> p (a b)"),
                                  b=U[hs, c * T:(c + 1) * T]: e.matmul(o, a, b, start=True, stop=True),
                                  reads=[bdb, Ub], writes=[ppb_])
                        W_ = {k: v[0] for k, v in wk.items()}; Wb_ = {k: v[1] for k, v in wk.items()}
                        tt("dve", W_["t1"][:], pr[:, 0:T], cost[:], ALU.mult, [prb, cosb], [Wb_["t1"]])
                        tt("dve", W_["t2"][:], pi_[:, 0:T], sint[:], ALU.mult, [pib, sinb], [Wb_["t2"]])
                        tt("dve", W_["t3"][:], pi_[:, 0:T], cost[:], ALU.mult, [pib, cosb], [Wb_["t3"]])
                        tt("dve", W_["t4"][:], pr[:, 0:T], sint[:], ALU.mult, [prb, sinb], [Wb_["t4"]])
                        tt("pool", W_["kr"][:], W_["t1"][:], W_["t2"][:], ALU.add, [Wb_["t1"], Wb_["t2"]], [Wb_["kr"]])
                        tt("pool", W_["ki"][:], W_["t3"][:], W_["t4"][:], ALU.subtract, [Wb_["t3"], Wb_["t4"]], [Wb_["ki"]])
                        for nm_k, nm_w, cc in (("kr", "wr", 0), ("ki", "wi", 1)):
                            kb.op("dve", lambda e, o=W_[nm_w], k_=W_[nm_k], cc=cc, q=q, rho=rho: e.tensor_tensor_scan(
                                out=o[:], data0=rho[:, q:q + 1].to_broadcast([128, T]), data1=k_[:], initial=cry[:, cc:cc + 1],
                                op0=ALU.mult, op1=ALU.add), reads=[Wb_[nm_k], rhob, cryb], writes=[Wb_[nm_w]])
                        tt("pool", W_["p1"][:], W_["wr"][:], cost[:], ALU.mult, [Wb_["wr"], cosb], [Wb_["p1"]])
                        tt("pool", W_["p2"][:], W_["wi"][:], sint[:], ALU.mult, [Wb_["wi"], sinb], [Wb_["p2"]])
                        tt("pool", W_["p3"][:], W_["wi"][:], cost[:], ALU.mult, [Wb_["wi"], cosb], [Wb_["p3"]])
                        tt("pool", W_["p4"][:], W_["wr"][:], sint[:], ALU.mult, [Wb_["wr"], sinb], [Wb_["p4"]])
                        tt("dve", W_["xr"][:], W_["p1"][:], W_["p2"][:], ALU.subtract, [Wb_["p1"], Wb_["p2"]], [Wb_["xr"]])
                        tt("dve", W_["xi"][:], W_["p3"][:], W_["p4"][:], ALU.add, [Wb_["p3"], Wb_["p4"]], [Wb_["xi"]])
                        kb.op("act", lambda e, a=W_["xr"]: e.copy(out=cry[:, 0:1], in_=a[:, T - 1:T]), reads=[Wb_["xr"]], writes=[cryb])
                        kb.op("act", lambda e, a=W_["xi"]: e.copy(out=cry[:, 1:2], in_=a[:, T - 1:T]), reads=[Wb_["xi"]], writes=[cryb])
                        py, pyb = bank()
                        for sub in range(nsub):
                            for ri, nm in ((0, "xr"), (1, "xi")):
                                cd, cdb = Cbd[(d, ri)]
                                kb.op("pe", lambda e, o=py[:, sub * 32:(sub + 1) * 32], a=W_[nm][:, sub * 128:(sub + 1) * 128],
                                      b=cd[:, q, :, :].rearrange("p a b -> p (a b)"), ri=ri: e.matmul(o, a, b, start=(ri == 0), stop=(ri == 1)),
                                      reads=[Wb_[nm], cdb], writes=[pyb])
                        kb.op("act", lambda e, o=Yacc[d][:, c * nsub:(c + 1) * nsub, q4 * 32:(q4 + 1) * 32],
                              a=py[:, 0:nsub * 32].rearrange("p (s c) -> p s c", c=32): e.copy(out=o, in_=a), reads=[pyb], writes=[Yaccb[d][c]])
            for b0 in range(0, nblk, nb4):
                i = ui % 2
                ui += 1
                kb.dma("sp", ub[i][:, 0:nb4, :], zd[b0 * 128:(b0 + nb4) * 128, ct * 128:(ct + 1) * 128].rearrange("(b t) c -> t b c", t=128),
                       src=zdb, dst=ubb[i])
                pf, pfb = bank()
                for bb in range(nb4):
                    b = b0 + bb
                    rb_ = nblk - 1 - b
                    kb.op("pe", lambda e, o=pf[:, bb * 128:(bb + 1) * 128], b_=Yacc[1][:, rb_, :]: e.matmul(o, Jt[:], b_, start=True, stop=False),
                          reads=[cb, Yaccb[1][rb_ // nsub]], writes=[pfb])
                    kb.op("pe", lambda e, o=pf[:, bb * 128:(bb + 1) * 128], b_=Yacc[0][:, b, :]: e.matmul(o, ident[:], b_, start=False, stop=True),
                          reads=[identb, Yaccb[0][b // nsub]], writes=[pfb])
                n_ = nb4 * 128
                S2, S2b = fin["s2"]; X2, X2b = fin["x2"]; SG, SGb = fin["sg"]; DU, DUb = fin["du"]
                kb.op("pool", lambda e, o=DU[:, 0:n_].rearrange("p (b c) -> p b c", c=128), a=ub[i][:, 0:nb4, :],
                      dd=Dt[:, ct * 128:(ct + 1) * 128].unsqueeze(1).to_broadcast([128, nb4, 128]): e.tensor_tensor(out=o, in0=a, in1=dd, op=ALU.mult),
                      reads=[ubb[i], Db], writes=[DUb])
                tt("dve", S2[:, 0:n_], pf[:, 0:n_], DU[:, 0:n_], ALU.add, [pfb, DUb], [S2b])
                kb.op("act", lambda e: e.activation(out=X2[:, 0:n_], in_=S2[:, 0:n_], func=AF.Square), reads=[S2b], writes=[X2b])
                kb.op("dve", lambda e: e.tensor_scalar(out=X2[:, 0:n_], in0=X2[:, 0:n_], scalar1=0.044715, scalar2=1.0, op0=ALU.mult, op1=ALU.add),
                      reads=[X2b], writes=[X2b])
                tt("pool", X2[:, 0:n_], X2[:, 0:n_], S2[:, 0:n_], ALU.mult, [X2b, S2b], [X2b])
                kb.op("act", lambda e: e.activation(out=SG[:, 0:n_], in_=X2[:, 0:n_], func=AF.Sigmoid, scale=2.0 * math.sqrt(2.0 / math.pi)),
                      reads=[X2b], writes=[SGb])
                tt("dve", yo[:, 0:nb4, :].rearrange("p b c -> p (b c)"), S2[:, 0:n_], SG[:, 0:n_], ALU.mult, [S2b, SGb], [yob])
                kb.dma("pool", yd[b0 * 128:(b0 + nb4) * 128, ct * 128:(ct + 1) * 128].rearrange("(b t) c -> t b c", t=128), yo[:, 0:nb4, :],
                       src=yob, dst=ydb)
        lc["end_phase"]()


    def na():
        lc["new_phase"]("na" + sname)
        C = lc["CST"]
        ident, identb = lc["ident"], lc["identb"]
        tbank, tbuf = lc["tbank"], lc["tbuf"]
        R = L // 64
        nblk = L // 128
        oh2 = sb("n_oh2", [62, 64, 128], BF16); mask2 = sb("n_mask2", [128, 64], F32); cb = kb.buf("n_const")
        kb.dma("sp", oh2[:], C["oh2"], dst=cb)
        kb.dma("sp", mask2[:], C["mask2"], dst=cb)
        RB = sb("n_rb", [62, 8, 14], F32); RBb = kb.buf("n_rb")
        for a in range(2):
            for h in range(8):
                kb.dma("sp", RB[a * 31:(a + 1) * 31, h, :], P["rel_bias"][l, h, a:a + 14, :].rearrange("dr m -> m dr"), dst=RBb)
        RBh = sb("n_rbh", [62, 8, 14], BF16); RBhb = kb.buf("n_rbh")
        kb.op("dve", lambda e: e.tensor_copy(out=RBh[:], in_=RB[:]), reads=[RBb], writes=[RBhb])
        BT = sb("n_bt", [128, 8, 14, 64], BF16); BTb = kb.buf("n_bt")
        for h in range(8):
            for half in range(2):
                bp, bpb = bank()
                for ql in range(32):
                    qc = half * 32 + ql
                    kb.op("pe", lambda e, o=bp[:, ql * 14:(ql + 1) * 14], a=oh2[:, qc, :], b=RBh[:, h, :]: e.matmul(o, a, b, start=True, stop=True),
                          reads=[cb, RBhb], writes=[bpb])
                kb.op("dve", lambda e, o=BT[:, h, :, half * 32:(half + 1) * 32], a=bp[:, 0:448].rearrange("p (q d) -> p d q", d=14),
                      mk=mask2[:, half * 32:(half + 1) * 32].unsqueeze(1).to_broadcast([128, 14, 32]): e.tensor_tensor(out=o, in0=a, in1=mk, op=ALU.add),
                      reads=[bpb, cb], writes=[BTb])
        QT = sb("n_QT", [128, L], BF16); QTb = kb.buf("n_QT")
        KT = sb("n_KT", [128, L], BF16); KTb = kb.buf("n_KT")
        Va = [sb("n_Va%d" % a, [128, nblk, 2, 65], BF16) for a in range(2)]; Vab = [kb.buf("n_Va%d" % a) for a in range(2)]
        ld = [sb("n_ld%d" % i, [128, 4, 128], BF16) for i in range(2)]; ldb = [kb.buf("n_ld%d" % i) for i in range(2)]
        PT = [sb("n_PT%d" % i, [128, 2, 4, 64], BF16) for i in range(2)]; PTb = [kb.buf("n_PT%d" % i) for i in range(2)]
        RF = 64 if R >= 64 else R
        Yst = sb("n_Y", [64, RF, 128], BF16); Ystb = kb.buf("n_Y")
        rec = sb("n_rec", [64, 2, 1], F32); recb = kb.buf("n_rec")
        nb4 = min(4, nblk)
        li = 0
        ti = 0
        pti = 0
        ydv = yd.rearrange("(r q) c -> q r c", q=64)
        for hp in range(4):
            for a in range(2):
                kb.op("pool", lambda e, a=a: e.memset(Va[a][:, :, :, 64:65], 1.0), writes=[Vab[a]])
            vcol = slice(3072 + hp * 128, 3072 + (hp + 1) * 128)
            for hh in range(2):
                vc0 = 3072 + hp * 128 + hh * 64
                bs_ = 32
                for b0 in range(0, nblk, bs_):
                    b1 = min(nblk, b0 + bs_)
                    kb.dma("sp", Va[0][:, b0:b1, hh, 0:64], zd[b0 * 128:b1 * 128, vc0:vc0 + 64].rearrange("(b t) d -> t b d", t=128), src=zdb, dst=Vab[0])
                    b1 = min(nblk - 1, b0 + bs_)
                    if b1 > b0:
                        kb.dma("sp", Va[1][:, b0:b1, hh, 0:64], zd[64 + b0 * 128:64 + b1 * 128, vc0:vc0 + 64].rearrange("(b t) d -> t b d", t=128), src=zdb, dst=Vab[1])
            for (dst, dstb, c0) in ((QT, QTb, 2048 + hp * 128), (KT, KTb, 2560 + hp * 128)):
                for b0 in range(0, nblk, nb4):
                    i = li % 2
                    li += 1
                    kb.dma("sp", ld[i][:, 0:nb4, :], zd[b0 * 128:(b0 + nb4) * 128, c0:c0 + 128].rearrange("(b t) c -> t b c", t=128), src=zdb, dst=ldb[i])
                    tbk, tbkb = tbank[ti % 2], tbuf[ti % 2]
                    ti += 1
                    for bb in range(nb4):
                        kb.op("pe", lambda e, o=tbk[:, bb * 128:(bb + 1) * 128], a=ld[i][:, bb, :]: e.transpose(out=o, in_=a, identity=ident[:]),
                              reads=[ldb[i], identb], writes=[tbkb])
                    kb.op("act", lambda e, o=dst[:, b0 * 128:(b0 + nb4) * 128], a=tbk[:, 0:nb4 * 128]: e.copy(out=o, in_=a), reads=[tbkb], writes=[dstb])
            for r in range(R):
                rs = min(max(r - 4, 0), R - 8)
                S, Sb = bank()
                Sv = S[:].rearrange("p (h b q) -> p h b q", h=2, b=4)
                for hh in range(2):
                    h = hp * 2 + hh
                    ps_ = slice(hh * 64, (hh + 1) * 64)
                    for b in range(4):
                        a0 = rs + 2 * b
                        dr0 = a0 - r + 7
                        kb.op("pe", lambda e, o=Sv[:, hh, b, :], a=KT[ps_, a0 * 64:a0 * 64 + 128], q_=QT[ps_, r * 64:(r + 1) * 64]:
                              e.matmul(o, a, q_, start=True, stop=False), reads=[KTb, QTb], writes=[Sb])
                        kb.op("pe", lambda e, o=Sv[:, hh, b, :], bt=BT[:, h, dr0, :]: e.matmul(o, ident[:], bt, start=False, stop=True),
                              reads=[identb, BTb], writes=[Sb])
                pi_ = pti % 2
                pti += 1
                kb.op("act", lambda e, o=PT[pi_][:].rearrange("p h b q -> p (h b q)"), a=S[:]: e.activation(out=o, in_=a, func=AF.Exp),
                      reads=[Sb], writes=[PTb[pi_]])
                O, Ob = bank()
                Ov = O[0:64, 0:130].rearrange("p (h c) -> p h c", c=65)
                for hh in range(2):
                    for b in range(4):
                        a0 = rs + 2 * b
                        al = a0 % 2
                        tix = a0 // 2
                        kb.op("pe", lambda e, o=Ov[:, hh, :], a=PT[pi_][:, hh, b, :], v=Va[al][:, tix, hh, :], b=b: e.matmul(o, a, v, start=(b == 0), stop=(b == 3)),
                              reads=[PTb[pi_], Vab[al]], writes=[Ob])
                kb.op("dve", lambda e, a=Ov[:, :, 64:65]: e.reciprocal(out=rec[:], in_=a), reads=[Ob], writes=[recb])
                rl = r % RF
                kb.op("dve", lambda e, o=Yst[:, rl, :].rearrange("p (h d) -> p h d", d=64), a=Ov[:, :, 0:64]:
                      e.tensor_tensor(out=o, in0=a, in1=rec[:].to_broadcast([64, 2, 64]), op=ALU.mult), reads=[Ob, recb], writes=[Ystb])
                if rl == RF - 1:
                    r0 = r - (RF - 1)
                    kb.dma("pool", ydv[:, r0:r0 + RF, 1024 + hp * 128:1024 + (hp + 1) * 128], Yst[:], src=Ystb, dst=ydb)
        lc["end_phase"]()

    if MIX_ENABLE["s5"]:
        s5()
    else:
        zero_cols(0, 512)
    if MIX_ENABLE["conv"]:
        conv()
    else:
        zero_cols(512, 1024)
    if MIX_ENABLE["na"]:
        na()
    else:
        zero_cols(1024, 1536)
    if MIX_ENABLE["fnet"]:
        fnet()
    else:
        zero_cols(1536, 2048)


class Buf:
    __slots__ = ("name", "w", "r", "sem", "cnt")

    def __init__(self, name):
        self.name = name
        self.w = None
        self.r = {}
        self.sem = None
        self.cnt = 0


class KB:
    def __init__(self, nc, es):
        self.nc = nc
        self.es = es
        self.eng = {"pe": nc.tensor, "act": nc.scalar, "dve": nc.vector, "pool": nc.gpsimd, "sp": nc.sync}
        self.lists = {k: [] for k in self.eng}
        self.sem = {k: es.enter_context(nc.semaphore("e_" + k)) for k in ("pe", "act", "dve", "pool")}
        self.cnt = {k: 0 for k in self.sem}
        self.seen = {k: {} for k in self.eng}
        self.dsems = {}
        self.owners = {}
        self.nb = 0

    def buf(self, name=None):
        self.nb += 1
        return Buf(name or f"b{self.nb}")

    def _dsem(self, b):
        if b.sem is None:
            nm = b.name
            own = self.owners.get(nm)
            if own is None:
                own = Buf("own_" + nm)
                own.sem = self.es.enter_context(self.nc.semaphore("d_%d_%s" % (len(self.owners), nm[:8])))
                self.owners[nm] = own
                self.dsems[id(own)] = own
            b.sem = own
        return b.sem

    def _waits(self, e, reads, writes):
        ev = []
        for b in reads:
            if b is not None and b.w is not None:
                ev.append(b.w)
        for b in writes:
            if b is None:
                continue
            if b.w is not None:
                ev.append(b.w)
            ev.extend(b.r.values())
        out = {}
        for x in ev:
            if x[0] == "c":
                _, pe, n = x
                if pe == e and e == "pe":
                    continue
                key = ("c", pe)
                sem = self.sem[pe]
                val = n
            else:
                _, owner = x
                key = ("d", id(owner))
                sem = owner.sem
                val = owner.cnt
            if self.seen[e].get(key, 0) >= val:
                continue
            if key not in out or out[key][1] < val:
                out[key] = (sem, val)
        for key, (sem, val) in out.items():
            self.seen[e][key] = val
            self.lists[e].append(("w", sem, val))

    def op(self, e, fn, reads=(), writes=()):
        self._waits(e, reads, writes)
        self.cnt[e] += 1
        n = self.cnt[e]
        self.lists[e].append(("i", fn, self.sem[e], 1))
        evt = ("c", e, n)
        for b in reads:
            if b is not None:
                b.r[("c", e)] = evt
        for b in writes:
            if b is not None:
                b.w = evt
                b.r = {}

    def dma(self, q, out, in_, src=None, dst=None):
        self._waits(q, [src], [dst])
        holder = dst if dst is not None else src
        owner = self._dsem(holder)
        owner.cnt += 16
        self.lists[q].append(("i", lambda eng, o=out, i=in_: eng.dma_start(out=o, in_=i), owner.sem, 16))
        evt = ("d", owner)
        if src is not None:
            src.r[("d", id(owner))] = evt
        if dst is not None:
            dst.w = evt
            dst.r = {}

    def barrier(self):
        for e in self.eng:
            for e2 in self.sem:
                if e2 == e:
                    continue
                key = ("c", e2)
                val = self.cnt[e2]
                if val > self.seen[e].get(key, 0):
                    self.seen[e][key] = val
                    self.lists[e].append(("w", self.sem[e2], val))
            for owner in self.dsems.values():
                key = ("d", id(owner))
                if owner.cnt > self.seen[e].get(key, 0):
                    self.seen[e][key] = owner.cnt
                    self.lists[e].append(("w", owner.sem, owner.cnt))

    def finish_wait(self, q, bufs):
        self._waits(q, bufs, bufs)

    def replay(self, block):
        def mk(name):
            lst = self.lists[name]

            def run(eng):
                for it in lst:
                    if it[0] == "w":
                        eng.wait_ge(it[1], it[2])
                    else:
                        it[1](eng).then_inc(it[2], it[3])
            return run
        block.tensor(mk("pe"))
        block.scalar(mk("act"))
        block.vector(mk("dve"))
        block.gpsimd(mk("pool"))
        block.sync(mk("sp"))


def _consts(L):
    N1 = 128
    N2 = L // 128
    n = np.arange(128)
    c = {}
    ang = 2 * np.pi * np.outer(n, n) / 128.0
    c["C1"], c["S1"] = np.cos(ang), np.sin(ang)
    n2 = np.arange(N2)
    ang = 2 * np.pi * np.outer(n2, n) / L
    c["Tc"], c["Ts"] = np.cos(ang), np.sin(ang)
    ang = 2 * np.pi * np.outer(n2, n2) / N2
    sc = 1.0 / math.sqrt(L * 128.0)
    c["C2"], c["S2n"] = np.cos(ang) * sc, -np.sin(ang) * sc
    return c


def build_program(LP, LS):
    nc = bass.Bass("TRN2", target_bir_lowering=False)
    seqs = [("p", LP), ("s", LS)]

    def din(name, shape, dt=F32):
        return nc.dram_tensor(name, list(shape), dt, kind="ExternalInput").ap()

    xin = {s: din("x_" + s, [L, D]) for s, L in seqs}
    pin = {s: din("p_" + s, [DEPTH, L, PLE]) for s, L in seqs}
    yout = {s: nc.dram_tensor("y_" + s, [L, D], F32, kind="ExternalOutput").ap() for s, L in seqs}
    g_mix = din("g_mix", [DEPTH, D]); g_ffn = din("g_ffn", [DEPTH, D]); g_ple = din("g_ple", [DEPTH, D])
    w_in = din("w_in", [DEPTH, D, MIX + 4 * D])
    lam_re = din("s5_lam_re", [DEPTH, 2, 32, 64]); lam_im = din("s5_lam_im", [DEPTH, 2, 32, 64])
    log_dt = din("s5_log_dt", [DEPTH, 2, 32])
    b_re = din("s5_b_re", [DEPTH, 2, 32, 64, 16]); b_im = din("s5_b_im", [DEPTH, 2, 32, 64, 16])
    c_re = din("s5_c_re", [DEPTH, 2, 32, 16, 64]); c_im = din("s5_c_im", [DEPTH, 2, 32, 16, 64])
    s5_d = din("s5_d", [DEPTH, 512]); w_glu = din("w_glu", [DEPTH, 512, 512]); conv_w = din("conv_w", [DEPTH, 3, 512])
    q_gain = din("q_gain", [DEPTH, 64]); k_gain = din("k_gain", [DEPTH, 64]); rel_bias = din("rel_bias", [DEPTH, 8, 15, 31])
    w_br = din("w_br", [DEPTH, 4, 512, D]); w_o = din("w_o", [DEPTH, D, D])
    w_ffn_in = din("w_ffn_in", [DEPTH, D, 2 * DFF]); w_ffn_out = din("w_ffn_out", [DEPTH, DFF, D])
    w_pg = din("w_ple_gate", [DEPTH, D, D]); w_pp = din("w_ple_proj", [DEPTH, PLE, D])
    PRM = dict(lam_re=lam_re, lam_im=lam_im, log_dt=log_dt, b_re=b_re, b_im=b_im, c_re=c_re, c_im=c_im, s5_d=s5_d,
               conv_w=conv_w, rel_bias=rel_bias)
    ident_d = din("c_ident", [128, 128], BF16)
    cs128_d = din("c_cs128", [128, 256], BF16)
    CST = dict(iota1=din("c_iota1", [128, 512]), maskh=din("c_maskh", [128, 4]), maskp=din("c_maskp", [128, 2]),
               maskpn=din("c_maskpn", [128, 2]), J=din("c_J", [128, 128], BF16), pi=din("c_pi", [128, 1]),
               oh2=din("c_oh2", [62, 64, 128], BF16), mask2=din("c_mask2", [128, 64]))
    ft = {}
    r1_d = din("c_r1", [128, 256], BF16); r2_d = din("c_r2", [128, 256], BF16)
    for s, L in seqs:
        N2 = L // 128
        ft[s] = dict(r1=r1_d, r2=r2_d,
                     tc=din("c_tc_" + s, [N2, 128]), ts=din("c_ts_" + s, [N2, 128]),
                     c2=din("c_c2_" + s, [N2, N2], BF16), s2n=din("c_s2n_" + s, [N2, N2], BF16))

    def dscr(name, shape, dt=BF16):
        return nc.dram_tensor(name, list(shape), dt).ap()

    W = {}
    for l in range(DEPTH):
        W[l] = dict(
            mix=dscr(f"wmix{l}", [8, 128, 16, 512]),
            gate=dscr(f"wgate{l}", [64, 128, 16, 128]),
            br=dscr(f"wbr{l}", [4, 16, 128, 4, 128]),
            glu=dscr(f"wglu{l}", [4, 128, 4, 128]),
            o=dscr(f"wo{l}", [4, 128, 16, 512]),
            f1=dscr(f"wf1{l}", [88, 128, 16, 128]),
            f2=dscr(f"wf2{l}", [4, 128, 44, 512]),
            pg=dscr(f"wpg{l}", [4, 128, 16, 512]),
            pp=dscr(f"wpp{l}", [4, 128, 2, 512]),
        )
    zbuf = {s: dscr("z_" + s, [L, ZC]) for s, L in seqs}
    ybuf = {s: dscr("ym_" + s, [L, D]) for s, L in seqs}
    xmid = {s: dscr("xm_" + s, [L, D], F32) for s, L in seqs}

    with ExitStack() as es:
        kb = KB(nc, es)

        cur = [es]

        nsb = [0]

        sbcache = {}

        def sb(name, shape, dt):
            key = (curkey[0], name, tuple(shape), str(dt))
            if key in sbcache:
                return sbcache[key]
            nsb[0] += 1
            t_ = cur[0].enter_context(nc.sbuf_tensor("%s_%d" % (name, nsb[0]), list(shape), dt))
            sbcache[key] = t_
            return t_

        curkey = [None]

        def new_phase(key=None):
            curkey[0] = key
            kb.barrier()
            if cur[0] is not es:
                cur[0].close()
            cur[0] = ExitStack()

        scopes = []

        def push_scope():
            scopes.append(cur[0])
            cur[0] = ExitStack()

        def pop_scope():
            kb.barrier()
            cur[0].close()
            cur[0] = scopes.pop()

        def end_phase():
            curkey[0] = None
            kb.barrier()
            if cur[0] is not es:
                cur[0].close()
            cur[0] = es

        def ps(name, shape, dt):
            return es.enter_context(nc.psum_tensor(name, list(shape), dt))

        pbank = [ps(f"pb{i}", [128, 512], F32) for i in range(6)]
        pbuf = [kb.buf(f"pb{i}") for i in range(6)]
        tbank = [ps(f"tb{i}", [128, 1024], BF16) for i in range(2)]
        tbuf = [kb.buf(f"tb{i}") for i in range(2)]
        prr = [0]

        def next_bank():
            i = prr[0] % 6
            prr[0] += 1
            return pbank[i], pbuf[i]

        trr = [0]

        def next_tbank():
            i = trr[0] % 2
            trr[0] += 1
            return tbank[i], tbuf[i]

        ident = sb("ident", [128, 128], BF16); identb = kb.buf("ident")
        kb.dma("sp", ident[:], ident_d, dst=identb)
        cs128 = sb("cs128", [128, 256], BF16); cs128b = kb.buf("cs128")
        kb.dma("sp", cs128[:], cs128_d, dst=cs128b)
        gv = sb("gvec", [128, 3 * DEPTH, 16], F32); gvb = kb.buf("gvec")
        qg = sb("qg", [128, DEPTH, 2, 512], F32); qgb = kb.buf("qg")
        new_phase()

        cv_in = [sb(f"cvin{i}", [128, 1024], F32) for i in range(2)]
        cv_inb = [kb.buf(f"cvin{i}") for i in range(2)]
        cv_out = [sb(f"cvout{i}", [128, 1024], BF16) for i in range(2)]
        cv_outb = [kb.buf(f"cvout{i}") for i in range(2)]
        cvi = [0]

        def convert(src, dst, nk, ncol):
            kstep = max(1, 1024 // ncol)
            for k0 in range(0, nk, kstep):
                k1 = min(nk, k0 + kstep)
                i = cvi[0] % 2
                cvi[0] += 1
                n = (k1 - k0) * ncol
                iv = cv_in[i][:, 0:n].rearrange("p (k c) -> p k c", c=ncol)
                ov = cv_out[i][:, 0:n].rearrange("p (k c) -> p k c", c=ncol)
                kb.dma("sp", iv, src[:, k0:k1, :], dst=cv_inb[i])
                e = "act" if (cvi[0] % 2) else "dve"
                if e == "act":
                    kb.op("act", lambda eng, o=cv_out[i][:, 0:n], a=cv_in[i][:, 0:n]: eng.copy(out=o, in_=a),
                          reads=[cv_inb[i]], writes=[cv_outb[i]])
                else:
                    kb.op("dve", lambda eng, o=cv_out[i][:, 0:n], a=cv_in[i][:, 0:n]: eng.tensor_copy(out=o, in_=a),
                          reads=[cv_inb[i]], writes=[cv_outb[i]])
                kb.dma("pool", dst[:, k0:k1, :], ov, src=cv_outb[i], dst=wscr)

        wscr = kb.buf("wscr")
        for l in range(DEPTH):
            win_v = w_in[l].rearrange("(k p) n -> p k n", p=128)
            for cb in range(8):
                convert(win_v[:, :, cb * 512:(cb + 1) * 512], W[l]["mix"][cb], 16, 512)
            for ct in range(64):
                convert(win_v[:, :, MIX + ct * 128: MIX + (ct + 1) * 128], W[l]["gate"][ct], 16, 128)
            for kbi in range(4):
                brv = w_br[l, kbi].rearrange("(k p) n -> p k n", p=128)
                for ct in range(16):
                    convert(brv[:, :, ct * 128:(ct + 1) * 128], W[l]["br"][kbi, ct], 4, 128)
            gluv = w_glu[l].rearrange("(k p) n -> p k n", p=128)
            for ct in range(4):
                convert(gluv[:, :, ct * 128:(ct + 1) * 128], W[l]["glu"][ct], 4, 128)
            ov_ = w_o[l].rearrange("(k p) n -> p k n", p=128)
            for cb in range(4):
                convert(ov_[:, :, cb * 512:(cb + 1) * 512], W[l]["o"][cb], 16, 512)
            f1v = w_ffn_in[l].rearrange("(k p) n -> p k n", p=128)
            for ct in range(88):
                convert(f1v[:, :, ct * 128:(ct + 1) * 128], W[l]["f1"][ct], 16, 128)
            f2v = w_ffn_out[l].rearrange("(k p) n -> p k n", p=128)
            for cb in range(4):
                convert(f2v[:, :, cb * 512:(cb + 1) * 512], W[l]["f2"][cb], 44, 512)
            pgv = w_pg[l].rearrange("(k p) n -> p k n", p=128)
            for cb in range(4):
                convert(pgv[:, :, cb * 512:(cb + 1) * 512], W[l]["pg"][cb], 16, 512)
            ppv = w_pp[l].rearrange("(k p) n -> p k n", p=128)
            for cb in range(4):
                convert(ppv[:, :, cb * 512:(cb + 1) * 512], W[l]["pp"][cb], 2, 512)

        kb.barrier()
        new_phase()
        xt = sb("xt", [128, 4, D], F32); xtb = [kb.buf(f"xt{i}") for i in range(4)]
        hb = [sb(f"h{i}", [128, D], BF16) for i in range(2)]; hbb = [kb.buf(f"h{i}") for i in range(2)]
        hT = sb("hT", [128, 16, TT], BF16); hTb = [kb.buf(f"hT{k}") for k in range(16)]
        for l in range(DEPTH):
            for j, g in enumerate((g_mix, g_ffn, g_ple)):
                kb.dma("sp", gv[:, l * 3 + j, :], g[l].rearrange("(k p) -> p k", p=128), dst=gvb)
        sq = sb("sqj", [128, D], BF16); sqb = kb.buf("sqj")
        st = sb("stat", [128, 8], F32); stb = kb.buf("stat")
        wb = [sb(f"wb{i}", [128, 22 * 512], BF16) for i in range(2)]; wbb = [kb.buf(f"wb{i}") for i in range(2)]
        wrr = [0]

        def load_w(src_ap, nk, ncol):
            i = wrr[0] % 2
            wrr[0] += 1
            v = wb[i][:, 0:nk * ncol].rearrange("p (k c) -> p k c", c=ncol)
            kb.dma("sp", v, src_ap, src=wscr, dst=wbb[i])
            return v, wbb[i]

        def norm_T(gidx, nsub=4):
            for s_ in range(nsub):
                i = s_ % 2
                kb.op("act", lambda eng, a=xt[:, s_, :]: eng.activation(out=sq[:], in_=a, func=AF.Square,
                                                                       accum_out=st[:, 0:1]),
                      reads=[xtb[s_]], writes=[sqb, stb])
                kb.op("dve", lambda eng: eng.tensor_scalar(out=st[:, 1:2], in0=st[:, 0:1], scalar1=1.0 / D,
                                                           scalar2=EPS, op0=ALU.mult, op1=ALU.add),
                      reads=[stb], writes=[stb])
                kb.op("act", lambda eng: eng.sqrt(out=st[:, 3:4], in_=st[:, 1:2]), reads=[stb], writes=[stb])
                kb.op("dve", lambda eng: eng.reciprocal(out=st[:, 2:3], in_=st[:, 3:4]), reads=[stb], writes=[stb])
                kb.op("dve", lambda eng, a=xt[:, s_, :], o=hb[i][:]: eng.tensor_scalar(
                    out=o, in0=a, scalar1=st[:, 2:3], scalar2=None, op0=ALU.mult),
                    reads=[xtb[s_], stb], writes=[hbb[i]])
                for k4 in range(4):
                    tb_, tbb_ = next_tbank()
                    for kk in range(4):
                        k = k4 * 4 + kk
                        kb.op("pe", lambda eng, o=tb_[:, kk * 128:(kk + 1) * 128], a=hb[i][:, k * 128:(k + 1) * 128]:
                              eng.transpose(out=o, in_=a, identity=ident[:]),
                              reads=[hbb[i], identb], writes=[tbb_])
                    for kk in range(4):
                        k = k4 * 4 + kk
                        kb.op("act", lambda eng, o=hT[:, k, s_ * 128:(s_ + 1) * 128], a=tb_[:, kk * 128:(kk + 1) * 128],
                              sc=gv[:, gidx, k:k + 1]: eng.activation(out=o, in_=a, func=AF.Copy, scale=sc),
                              reads=[tbb_, gvb], writes=[hTb[k]])

        outb = kb.buf("outb")
        xmb = {s: kb.buf("xm_" + s) for s, _ in seqs}
        zt = [sb(f"zt{i}", [128, 512], BF16) for i in range(2)]; ztb = [kb.buf(f"zt{i}") for i in range(2)]
        zrr = [0]
        for l in range(DEPTH):
            for j, g in enumerate((q_gain, k_gain)):
                for hh in range(8):
                    kb.dma("sp", qg[:, l, j, hh * 64:(hh + 1) * 64], g[l:l + 1, :].partition_broadcast(128) if False else g[l:l + 1, :].to_broadcast([128, 64]), dst=qgb)
        kb.op("dve", lambda eng: eng.tensor_scalar(out=qg[:, :, 0, :], in0=qg[:, :, 0, :], scalar1=0.125, scalar2=None, op0=ALU.mult),
              reads=[qgb], writes=[qgb])
        qs = sb("qs", [128, 512], F32); qsb = kb.buf("qs")
        qst = sb("qst", [128, 24], F32); qstb = kb.buf("qst")
        udT = sb("udT", [128, 4, TT], BF16); udTb = [kb.buf(f"udT{g}") for g in range(4)]

        def pass1(sname, L, l, xsrc):
            zd = zbuf[sname]
            for t in range(L // TT):
                t0 = t * TT
                for s_ in range(4):
                    kb.dma("sp", xt[:, s_, :], xsrc[t0 + s_ * 128: t0 + (s_ + 1) * 128, :], src=(xmb[sname] if l > 0 else None), dst=xtb[s_])
                norm_T(l * 3 + 0)
                for cb in range(8):
                    wv, wvb = load_w(W[l]["mix"][cb], 16, 512)
                    if cb < 7:
                        for s_ in range(4):
                            pb_, pbb_ = next_bank()
                            for k in range(16):
                                kb.op("pe", lambda eng, o=pb_[:], a=hT[:, k, s_ * 128:(s_ + 1) * 128], b=wv[:, k, :], k=k:
                                      eng.matmul(o, a, b, start=(k == 0), stop=(k == 15)),
                                      reads=[hTb[k], wvb], writes=[pbb_])
                            zi = zrr[0] % 2
                            zrr[0] += 1
                            if cb in (4, 5):
                                j = cb - 4
                                kb.op("act", lambda eng, a=pb_[:]: eng.activation(out=qs[:], in_=a, func=AF.Square),
                                      reads=[pbb_], writes=[qsb])
                                kb.op("dve", lambda eng: eng.tensor_reduce(out=qst[:, 0:8], in_=qs[:].rearrange("p (h d) -> p h d", d=64),
                                                                           axis=AX.X, op=ALU.add),
                                      reads=[qsb], writes=[qstb])
                                kb.op("dve", lambda eng: eng.tensor_scalar(out=qst[:, 8:16], in0=qst[:, 0:8], scalar1=1.0 / 64, scalar2=EPS,
                                                                           op0=ALU.mult, op1=ALU.add), reads=[qstb], writes=[qstb])
                                kb.op("act", lambda eng: eng.sqrt(out=qst[:, 16:24], in_=qst[:, 8:16]), reads=[qstb], writes=[qstb])
                                kb.op("dve", lambda eng: eng.reciprocal(out=qst[:, 0:8], in_=qst[:, 16:24]), reads=[qstb], writes=[qstb])
                                kb.op("dve", lambda eng, a=pb_[:]: eng.tensor_tensor(
                                    out=qs[:].rearrange("p (h d) -> p h d", d=64), in0=a.rearrange("p (h d) -> p h d", d=64),
                                    in1=qst[:, 0:8].unsqueeze(2).to_broadcast([128, 8, 64]), op=ALU.mult),
                                    reads=[pbb_, qstb], writes=[qsb])
                                kb.op("dve", lambda eng, o=zt[zi][:], g_=qg[:, l, j, :]: eng.tensor_tensor(out=o, in0=qs[:], in1=g_, op=ALU.mult),
                                      reads=[qsb, qgb], writes=[ztb[zi]])
                            else:
                                kb.op("act", lambda eng, o=zt[zi][:], a=pb_[:]: eng.copy(out=o, in_=a), reads=[pbb_], writes=[ztb[zi]])
                            kb.dma("pool", zd[t0 + s_ * 128: t0 + (s_ + 1) * 128, cb * 512:(cb + 1) * 512], zt[zi][:],
                                   src=ztb[zi], dst=zdb[sname])
                    else:
                        for g_ in range(4):
                            pb_, pbb_ = next_bank()
                            for k in range(16):
                                kb.op("pe", lambda eng, o=pb_[:], a=wv[:, k, g_ * 128:(g_ + 1) * 128], b=hT[:, k, :], k=k:
                                      eng.matmul(o, a, b, start=(k == 0), stop=(k == 15)),
                                      reads=[hTb[k], wvb], writes=[pbb_])
                            kb.op("act", lambda eng, o=udT[:, g_, :], a=pb_[:]: eng.copy(out=o, in_=a), reads=[pbb_], writes=[udTb[g_]])
                        for s_ in range(4):
                            for gp in range(2):
                                pb_, pbb_ = next_bank()
                                for gg in range(2):
                                    g_ = gp * 2 + gg
                                    kb.op("pe", lambda eng, o=pb_[:, gg * 256:(gg + 1) * 256], a=udT[:, g_, s_ * 128:(s_ + 1) * 128]:
                                          eng.matmul(o, a, cs128[:], start=True, stop=True),
                                          reads=[udTb[g_], cs128b], writes=[pbb_])
                                zi = zrr[0] % 2
                                zrr[0] += 1
                                kb.op("act", lambda eng, o=zt[zi][:], a=pb_[:]: eng.copy(out=o, in_=a), reads=[pbb_], writes=[ztb[zi]])
                                kb.dma("pool", zd[t0 + s_ * 128: t0 + (s_ + 1) * 128, 3584 + gp * 512: 3584 + (gp + 1) * 512], zt[zi][:],
                                       src=ztb[zi], dst=zdb[sname])

        zdb = {s: kb.buf("zd_" + s) for s, _ in seqs}
        ydb = {s: kb.buf("yd_" + s) for s, _ in seqs}
        big = sb("big", [128, 44, TT], BF16)
        bigb = [kb.buf(f"big{k}") for k in range(44)]
        mT = sb("mT", [128, 16, TT], BF16); mTb = [kb.buf(f"mT{k}") for k in range(16)]
        macc = sb("macc", [128, TT], F32); maccb = kb.buf("macc")
        gt = [sb(f"gt{i}", [128, TT], F32) for i in range(2)]; gtb = [kb.buf(f"gt{i}") for i in range(2)]
        grr = [0]
        ysub = sb("ysub", [128, D], BF16); ysubb = kb.buf("ysub")
        pt = sb("ptile", [128, PLE], F32); ptb = kb.buf("ptile")
        ptb16 = sb("ptb16", [128, PLE], BF16); ptb16b = kb.buf("ptb16")
        pT = sb("pT", [128, 2, TT], BF16); pTb = kb.buf("pT")
        end_phase()

        def gtile():
            i = grr[0] % 2
            grr[0] += 1
            return gt[i], gtb[i]

        def mm_group(pb_, pbb_, pairs):
            n = len(pairs)
            for i, (a, ab, b, bb) in enumerate(pairs):
                kb.op("pe", lambda eng, o=pb_, a=a, b=b, i=i: eng.matmul(o, a, b, start=(i == 0), stop=(i == n - 1)),
                      reads=[ab, bb], writes=[pbb_])

        def pass3(sname, L, l, xsrc, xdst, xdstb):
            yd = ybuf[sname]
            for t in range(L // TT):
                t0 = t * TT
                for s_ in range(4):
                    kb.dma("sp", xt[:, s_, :], xsrc[t0 + s_ * 128: t0 + (s_ + 1) * 128, :], src=(xmb[sname] if l > 0 else None), dst=xtb[s_])
                norm_T(l * 3 + 0)
                for s_ in range(4):
                    kb.dma("sp", ysub[:], yd[t0 + s_ * 128: t0 + (s_ + 1) * 128, :], src=ydb[sname], dst=ysubb)
                    for k4 in range(4):
                        tb_, tbb_ = next_tbank()
                        for kk in range(4):
                            k = k4 * 4 + kk
                            kb.op("pe", lambda eng, o=tb_[:, kk * 128:(kk + 1) * 128], a=ysub[:, k * 128:(k + 1) * 128]:
                                  eng.transpose(out=o, in_=a, identity=ident[:]), reads=[ysubb, identb], writes=[tbb_])
                        for kk in range(4):
                            k = k4 * 4 + kk
                            kb.op("dve", lambda eng, o=big[:, k, s_ * 128:(s_ + 1) * 128], a=tb_[:, kk * 128:(kk + 1) * 128]:
                                  eng.tensor_copy(out=o, in_=a), reads=[tbb_], writes=[bigb[k]])
                sig = []
                for co in range(4):
                    wv, wvb = load_w(W[l]["glu"][co], 4, 128)
                    pb_, pbb_ = next_bank()
                    mm_group(pb_[:], pbb_, [(wv[:, ci, :], wvb, big[:, ci, :], bigb[ci]) for ci in range(4)])
                    g_, gb_ = gtile()
                    kb.op("act", lambda eng, o=g_[:], a=pb_[:]: eng.activation(out=o, in_=a, func=AF.Sigmoid), reads=[pbb_], writes=[gb_])
                    sig.append((g_, gb_))
                    if co % 2 == 1:
                        for cc in (co - 1, co):
                            g2, gb2 = sig[cc]
                            kb.op("dve", lambda eng, o=mT[:, cc, :], a=big[:, cc, :], b=g2[:]: eng.tensor_tensor(out=o, in0=a, in1=b, op=ALU.mult),
                                  reads=[bigb[cc], gb2], writes=[mTb[cc]])
                for cc in range(4):
                    kb.op("dve", lambda eng, o=big[:, cc, :], a=mT[:, cc, :]: eng.tensor_copy(out=o, in_=a), reads=[mTb[cc]], writes=[bigb[cc]])
                for ct in range(16):
                    for kbi in range(4):
                        wv, wvb = load_w(W[l]["gate"][kbi * 16 + ct], 16, 128)
                        pg_, pgb_ = next_bank()
                        mm_group(pg_[:], pgb_, [(wv[:, k, :], wvb, hT[:, k, :], hTb[k]) for k in range(16)])
                        wv2, wvb2 = load_w(W[l]["br"][kbi, ct], 4, 128)
                        pp_, ppb_ = next_bank()
                        mm_group(pp_[:], ppb_, [(wv2[:, c, :], wvb2, big[:, kbi * 4 + c, :], bigb[kbi * 4 + c]) for c in range(4)])
                        g_, gb_ = gtile()
                        kb.op("act", lambda eng, o=g_[:], a=pg_[:]: eng.activation(out=o, in_=a, func=AF.Sigmoid), reads=[pgb_], writes=[gb_])
                        if kbi == 0:
                            kb.op("dve", lambda eng, a=g_[:], b=pp_[:]: eng.tensor_tensor(out=macc[:], in0=b, in1=a, op=ALU.mult),
                                  reads=[gb_, ppb_], writes=[maccb])
                        else:
                            kb.op("dve", lambda eng, a=g_[:], b=pp_[:]: eng.tensor_tensor(out=a, in0=b, in1=a, op=ALU.mult),
                                  reads=[gb_, ppb_], writes=[gb_])
                            if kbi < 3:
                                kb.op("dve", lambda eng, a=g_[:]: eng.tensor_tensor(out=macc[:], in0=macc[:], in1=a, op=ALU.add),
                                      reads=[gb_, maccb], writes=[maccb])
                            else:
                                kb.op("dve", lambda eng, a=g_[:], o=mT[:, ct, :]: eng.tensor_tensor(out=o, in0=macc[:], in1=a, op=ALU.add),
                                      reads=[gb_, maccb], writes=[mTb[ct]])

                def tok_major(wkey, nk, srcT, srcTb, fin, khalf=None):
                    for cb in range(4):
                        halves = [(0, nk)] if khalf is None else [(0, khalf), (khalf, nk)]
                        wvs = []
                        for (k0, k1) in halves:
                            wv, wvb = load_w(W[l][wkey][cb][:, k0:k1, :], k1 - k0, 512)
                            wvs.append((k0, k1, wv, wvb))
                        for s_ in range(4):
                            pb_, pbb_ = next_bank()
                            pairs = []
                            for (k0, k1, wv, wvb) in wvs:
                                for k in range(k0, k1):
                                    pairs.append((srcT[:, k, s_ * 128:(s_ + 1) * 128], srcTb[k], wv[:, k - k0, :], wvb))
                            mm_group(pb_[:], pbb_, pairs)
                            fin(cb, s_, pb_, pbb_)

                def add_resid(cb, s_, pb_, pbb_):
                    kb.op("dve", lambda eng, x_=xt[:, s_, cb * 512:(cb + 1) * 512], a=pb_[:]: eng.tensor_tensor(out=x_, in0=a, in1=x_, op=ALU.add),
                          reads=[pbb_, xtb[s_]], writes=[xtb[s_]])

                tok_major("o", 16, mT, mTb, add_resid)
                norm_T(l * 3 + 1)
                for ft_ in range(44):
                    wva, wvab = load_w(W[l]["f1"][ft_], 16, 128)
                    pa_, pab_ = next_bank()
                    mm_group(pa_[:], pab_, [(wva[:, k, :], wvab, hT[:, k, :], hTb[k]) for k in range(16)])
                    wvb_, wvbb_ = load_w(W[l]["f1"][44 + ft_], 16, 128)
                    pb2_, pb2b_ = next_bank()
                    mm_group(pb2_[:], pb2b_, [(wvb_[:, k, :], wvbb_, hT[:, k, :], hTb[k]) for k in range(16)])
                    g_, gb_ = gtile()
                    kb.op("act", lambda eng, o=g_[:], a=pa_[:]: eng.activation(out=o, in_=a, func=AF.Silu), reads=[pab_], writes=[gb_])
                    kb.op("dve", lambda eng, o=big[:, ft_, :], a=g_[:], b=pb2_[:]: eng.tensor_tensor(out=o, in0=b, in1=a, op=ALU.mult),
                          reads=[gb_, pb2b_], writes=[bigb[ft_]])
                tok_major("f2", 44, big, bigb, add_resid, khalf=22)
                norm_T(l * 3 + 2)
                for s_ in range(4):
                    kb.dma("sp", pt[:], pin[sname][l, t0 + s_ * 128:t0 + (s_ + 1) * 128, :], dst=ptb)
                    kb.op("dve", lambda eng: eng.tensor_copy(out=ptb16[:], in_=pt[:]), reads=[ptb], writes=[ptb16b])
                    tb_, tbb_ = next_tbank()
                    for kk in range(2):
                        kb.op("pe", lambda eng, o=tb_[:, kk * 128:(kk + 1) * 128], a=ptb16[:, kk * 128:(kk + 1) * 128]:
                              eng.transpose(out=o, in_=a, identity=ident[:]), reads=[ptb16b, identb], writes=[tbb_])
                    kb.op("dve", lambda eng, o=pT[:, :, s_ * 128:(s_ + 1) * 128], a=tb_[:, 0:256].rearrange("p (k c) -> p k c", c=128):
                          eng.tensor_copy(out=o, in_=a), reads=[tbb_], writes=[pTb])
                for cb in range(4):
                    wv, wvb = load_w(W[l]["pg"][cb], 16, 512)
                    wv2, wvb2 = load_w(W[l]["pp"][cb], 2, 512)
                    for s_ in range(4):
                        pg_, pgb_ = next_bank()
                        mm_group(pg_[:], pgb_, [(hT[:, k, s_ * 128:(s_ + 1) * 128], hTb[k], wv[:, k, :], wvb) for k in range(16)])
                        pp_, ppb_ = next_bank()
                        mm_group(pp_[:], ppb_, [(pT[:, k, s_ * 128:(s_ + 1) * 128], pTb, wv2[:, k, :], wvb2) for k in range(2)])
                        g_, gb_ = gtile()
                        kb.op("act", lambda eng, o=g_[:], a=pg_[:]: eng.activation(out=o, in_=a, func=AF.Sigmoid), reads=[pgb_], writes=[gb_])
                        kb.op("dve", lambda eng, a=g_[:], b=pp_[:]: eng.tensor_tensor(out=a, in0=b, in1=a, op=ALU.mult),
                              reads=[gb_, ppb_], writes=[gb_])
                        kb.op("dve", lambda eng, x_=xt[:, s_, cb * 512:(cb + 1) * 512], a=g_[:]: eng.tensor_tensor(out=x_, in0=a, in1=x_, op=ALU.add),
                              reads=[gb_, xtb[s_]], writes=[xtb[s_]])
                for s_ in range(4):
                    kb.dma("pool", xdst[t0 + s_ * 128: t0 + (s_ + 1) * 128, :], xt[:, s_, :], src=xtb[s_], dst=xdstb)

        def mixers(sname, L, l):
            MIXERS(kb, nc, sb, sname, L, l, zbuf[sname], zdb[sname], ybuf[sname], ydb[sname], locals_=dict(
                next_bank=next_bank, next_tbank=next_tbank, ident=ident, identb=identb, new_phase=new_phase,
                end_phase=end_phase, push_scope=push_scope, pop_scope=pop_scope, P=PRM, CST=CST, ft=ft[sname], pbank=pbank, pbuf=pbuf, tbank=tbank, tbuf=tbuf))

        for sname, L in seqs:
            for l in range(NLAYER):
                xsrc = xin[sname] if l == 0 else xmid[sname]
                last = (l == NLAYER - 1)
                xdst = yout[sname] if last else xmid[sname]
                xdstb = outb if last else xmb[sname]
                kb.barrier()
                pass1(sname, L, l, xsrc)
                kb.barrier()
                mixers(sname, L, l)
                kb.barrier()
                if DEBUG_Y:
                    new_phase("dbg")
                    dbg = sb("dbg", [128, D], BF16); dbgb = kb.buf("dbg")
                    dbgf = sb("dbgf", [128, D], F32); dbgfb = kb.buf("dbgf")
                    for t in range(L // 128):
                        kb.dma("sp", dbg[:], ybuf[sname][t * 128:(t + 1) * 128, :], src=ydb[sname], dst=dbgb)
                        kb.op("dve", lambda eng: eng.tensor_copy(out=dbgf[:], in_=dbg[:]), reads=[dbgb], writes=[dbgfb])
                        kb.dma("pool", yout[sname][t * 128:(t + 1) * 128, :], dbgf[:], src=dbgfb, dst=outb)
                    end_phase()
                    continue
                pass3(sname, L, l, xsrc, xdst, xdstb)
        kb.finish_wait("pool", [outb, wscr])
        with nc.allow_non_contiguous_dma(reason="small strided tables"), nc.Block() as block:
            kb.replay(block)
    return nc


def _bf(a):
    return np.ascontiguousarray(np.asarray(a, np.float32).astype(ml_dtypes.bfloat16))


def make_in_map(inp, xp, pp, xs, ps_, LP, LS):
    m = {"x_p": np.ascontiguousarray(xp), "p_p": np.ascontiguousarray(pp), "x_s": np.ascontiguousarray(xs),
         "p_s": np.ascontiguousarray(ps_)}
    for k in ("g_mix", "g_ffn", "g_ple", "w_in", "s5_lam_re", "s5_lam_im", "s5_log_dt", "s5_b_re", "s5_b_im", "s5_c_re",
              "s5_c_im", "s5_d", "w_glu", "conv_w", "q_gain", "k_gain", "rel_bias", "w_br", "w_o", "w_ffn_in", "w_ffn_out",
              "w_ple_gate", "w_ple_proj"):
        m[k] = np.ascontiguousarray(np.asarray(inp[k], np.float32))
    m["c_ident"] = _bf(np.eye(128))
    m["c_J"] = _bf(np.eye(128)[::-1])
    m["c_iota1"] = np.ascontiguousarray(np.tile(np.arange(1, 513, dtype=np.float32)[None, :], (128, 1)))
    pidx = np.arange(128)
    m["c_maskh"] = np.ascontiguousarray(np.stack([(((pidx // 32) % 2) == sel) & (((pidx // 16) % 2) == g2)
                                                  for sel in range(2) for g2 in range(2)], axis=1).astype(np.float32))
    m["c_maskp"] = np.ascontiguousarray(np.stack([(pidx // 64) == 0, (pidx // 64) == 1], axis=1).astype(np.float32))
    m["c_maskpn"] = -m["c_maskp"]
    m["c_pi"] = np.full((128, 1), math.pi, np.float32)
    qc = np.arange(64); kc = np.arange(64)
    cs_ = np.clip(qc - 8, 0, 48)
    valid = (kc[:, None] >= cs_[None, :]) & (kc[:, None] < cs_[None, :] + 16)
    midx = kc[:, None] - qc[None, :] + 15
    oh = np.zeros((2, 31, 64, 2, 64), np.float32)
    for a in range(2):
        for k_ in range(64):
            for q_ in range(64):
                if valid[k_, q_]:
                    oh[a, midx[k_, q_], q_, a, k_] = 1.0
    m["c_oh2"] = _bf(oh.reshape(62, 64, 128))
    mk = np.where(valid, 0.0, -30000.0).astype(np.float32)
    m["c_mask2"] = np.ascontiguousarray(np.concatenate([mk, mk], axis=0))
    n = np.arange(128)
    ang = 2 * np.pi * np.outer(n, n) / 128.0
    C1, S1 = np.cos(ang), np.sin(ang)
    m["c_cs128"] = _bf(np.concatenate([C1, S1], axis=1))
    m["c_r1"] = _bf(np.concatenate([C1, S1], axis=1))
    m["c_r2"] = _bf(np.concatenate([-S1, C1], axis=1))
    for s, L in (("p", LP), ("s", LS)):
        c = _consts(L)
        m["c_tc_" + s] = np.ascontiguousarray(c["Tc"].astype(np.float32))
        m["c_ts_" + s] = np.ascontiguousarray(c["Ts"].astype(np.float32))
        m["c_c2_" + s] = _bf(c["C2"])
        m["c_s2n_" + s] = _bf(c["S2n"])
    return m


_PROG = {}


def kernel(**inputs):
    LP, LS = 16384, 4096
    xp = np.asarray(inputs["x_prompt"], np.float32)
    xs = np.asarray(inputs["x_sample"], np.float32)
    pp = np.asarray(inputs["p_prompt"], np.float32)
    ps_ = np.asarray(inputs["p_sample"], np.float32)
    if "nc" not in _PROG:
        _PROG["nc"] = build_program(LP, LS)
    nc = _PROG["nc"]
    zxp = np.zeros((LP, D), np.float32); zpp = np.zeros((DEPTH, LP, PLE), np.float32)
    zxs = np.zeros((LS, D), np.float32); zps = np.zeros((DEPTH, LS, PLE), np.float32)
    in_maps = []
    for c in range(8):
        a = (xp[c], pp[:, c]) if c < 2 else (zxp, zpp)
        b = (xs[c], ps_[:, c]) if c < 4 else (zxs, zps)
        in_maps.append(make_in_map(inputs, a[0], a[1], b[0], b[1], LP, LS))
    res = run_bass_kernel_spmd(nc, in_maps, core_ids=list(range(8)))
    yp = np.stack([np.asarray(res.results[c]["y_p"], np.float32) for c in range(2)], axis=0)
    ys = np.stack([np.asarray(res.results[c]["y_s"], np.float32) for c in range(4)], axis=0)
    return (yp, ys)
```

```python
import math
from contextlib import ExitStack
import numpy as np
import ml_dtypes
import concourse.bass as bass
import concourse.mybir as mybir
from concourse.bass_utils import run_bass_kernel_spmd

F32 = mybir.dt.float32
BF16 = mybir.dt.bfloat16
AF = mybir.ActivationFunctionType
ALU = mybir.AluOpType
AX = mybir.AxisListType

D = 2048
DEPTH = 2
PLE = 256
DFF = 5632
MIX = 4096
ZC = 3584 + 1024
EPS = 1e-6
TT = 512
DBG0 = 0
NLAYER = DEPTH


DEBUG_Y = False
MIX_ENABLE = dict(conv=True, fnet=True, s5=True, na=True)


def MIXERS(kb, nc, sb, sname, L, l, zd, zdb, yd, ydb, locals_):
    lc = locals_
    P = lc["P"]
    pbank, pbuf = lc["pbank"], lc["pbuf"]
    prr = [0]

    def bank():
        i = prr[0] % 6
        prr[0] += 1
        return pbank[i], pbuf[i]

    def zero_cols(c0, c1):
        lc["new_phase"]("zero")
        t = sb("mixz", [128, 512], BF16)
        b = kb.buf("mixz")
        kb.op("pool", lambda eng: eng.memset(t[:], 0.0), writes=[b])
        for r in range(L // 128):
            kb.dma("pool", yd[r * 128:(r + 1) * 128, c0:c1], t[:, 0:c1 - c0], src=b, dst=ydb)
        lc["end_phase"]()

    def conv():
        lc["new_phase"]("conv" + sname)
        n = L // 128
        nb = min(n, 16)
        zd3 = zd.rearrange("(p n) c -> p n c", n=n)
        yd3 = yd.rearrange("(p n) c -> p n c", n=n)
        CV = sb("cv", [128, nb + 2, 1024], BF16); CVb = kb.buf("cv")
        BG = sb("bg", [128, nb, 512], BF16); BGb = kb.buf("bg")
        Z = sb("cz", [128, nb + 2, 512], F32); Zb = kb.buf("cz")
        ACC = sb("cacc", [128, nb, 512], F32); ACCb = kb.buf("cacc")
        T1 = sb("ct1", [128, nb, 512], F32); T1b = kb.buf("ct1")
        YB = sb("cyb", [128, nb, 512], BF16); YBb = kb.buf("cyb")
        Wt = sb("cw", [128, 3, 512], F32); Wb = kb.buf("cw")
        kb.dma("sp", Wt[:], P["conv_w"][l:l + 1, :, :].to_broadcast([128, 3, 512]), dst=Wb)
        for j0 in range(0, n, nb):
            kb.dma("sp", CV[:, 1:nb + 1, :], zd3[:, j0:j0 + nb, 1024:2048], src=zdb, dst=CVb)
            if j0 > 0:
                kb.dma("sp", CV[:, 0:1, :], zd3[:, j0 - 1:j0, 1024:2048], src=zdb, dst=CVb)
            else:
                kb.op("pool", lambda eng: eng.memset(CV[:, 0:1, :], 0.0), writes=[CVb])
                kb.dma("sp", CV[1:128, 0:1, :], zd3[0:127, n - 1:n, 1024:2048], src=zdb, dst=CVb)
            if j0 + nb < n:
                kb.dma("sp", CV[:, nb + 1:nb + 2, :], zd3[:, j0 + nb:j0 + nb + 1, 1024:2048], src=zdb, dst=CVb)
            else:
                kb.op("pool", lambda eng: eng.memset(CV[:, nb + 1:nb + 2, :], 0.0), writes=[CVb])
                kb.dma("sp", CV[0:127, nb + 1:nb + 2, :], zd3[1:128, 0:1, 1024:2048], src=zdb, dst=CVb)
            kb.dma("sp", BG[:], zd3[:, j0:j0 + nb, 512:1024], src=zdb, dst=BGb)
            kb.op("dve", lambda eng: eng.tensor_tensor(out=Z[:], in0=CV[:, :, 0:512], in1=CV[:, :, 512:1024], op=ALU.mult),
                  reads=[CVb], writes=[Zb])

            def wb_(k):
                return Wt[:, k:k + 1, :].to_broadcast([128, nb, 512])
            kb.op("dve", lambda eng: eng.tensor_tensor(out=ACC[:], in0=Z[:, 0:nb, :], in1=wb_(0), op=ALU.mult),
                  reads=[Zb, Wb], writes=[ACCb])
            for k in (1, 2):
                kb.op("pool", lambda eng, k=k: eng.tensor_tensor(out=T1[:], in0=Z[:, k:k + nb, :], in1=wb_(k), op=ALU.mult),
                      reads=[Zb, Wb], writes=[T1b])
                kb.op("dve", lambda eng: eng.tensor_tensor(out=ACC[:], in0=ACC[:], in1=T1[:], op=ALU.add),
                      reads=[T1b, ACCb], writes=[ACCb])
            kb.op("dve", lambda eng: eng.tensor_tensor(out=YB[:], in0=ACC[:], in1=BG[:], op=ALU.mult),
                  reads=[ACCb, BGb], writes=[YBb])
            kb.dma("pool", yd3[:, j0:j0 + nb, 512:1024], YB[:], src=YBb, dst=ydb)
        lc["end_phase"]()

    def fnet():
        lc["new_phase"]("fnet" + sname)
        ft = lc["ft"]
        N2 = L // 128
        V = sb("fV", [128, N2, 256], BF16); Vb = kb.buf("fV")
        Y = sb("fY", [N2, 128, 128], BF16); Yb = kb.buf("fY")
        r1 = sb("fr1", [128, 256], BF16); r2 = sb("fr2", [128, 256], BF16); rb = kb.buf("fr")
        tcd = sb("ftc", [N2, 256], F32); tsd = sb("fts", [N2, 256], F32); tb_ = kb.buf("ft")
        c2 = sb("fc2", [N2, N2], BF16); s2n = sb("fs2", [N2, N2], BF16); cb_ = kb.buf("fc")
        P1 = [sb(f"fp1{i}", [N2, 2, 256], F32) for i in range(2)]; P1b = [kb.buf(f"fp1{i}") for i in range(2)]
        P2 = [sb(f"fp2{i}", [N2, 2, 256], F32) for i in range(2)]; P2b = [kb.buf(f"fp2{i}") for i in range(2)]
        APB = [sb(f"fab{i}", [N2, 4, 256], BF16) for i in range(2)]; APBb = [kb.buf(f"fab{i}") for i in range(2)]
        kb.dma("sp", r1[:], ft["r1"], dst=rb); kb.dma("sp", r2[:], ft["r2"], dst=rb)
        for h in range(2):
            kb.dma("sp", tcd[:, h * 128:(h + 1) * 128], ft["tc"], dst=tb_)
            kb.dma("sp", tsd[:, h * 128:(h + 1) * 128], ft["ts"], dst=tb_)
        kb.dma("sp", c2[:], ft["c2"], dst=cb_); kb.dma("sp", s2n[:], ft["s2n"], dst=cb_)
        zv = zd.rearrange("(a b) c -> a b c", b=N2)
        yv = yd.rearrange("(k2 k1) c -> k2 k1 c", k1=128)
        cnt = 0
        for g in range(4):
            for h0 in range(0, 128, 32):
                kb.dma("sp", V[h0:h0 + 32], zv[h0:h0 + 32, :, 3584 + g * 256: 3584 + (g + 1) * 256], src=zdb, dst=Vb)
            for j4 in range(0, 128, 4):
                ai = (j4 // 4) % 2
                for hp in range(2):
                    pi = cnt % 2
                    cnt += 1
                    A, Ab = bank()
                    for jj in range(2):
                        j = j4 + hp * 2 + jj
                        kb.op("pe", lambda eng, o=A[0:N2, jj * 256:(jj + 1) * 256], a=V[:, :, j]: eng.matmul(o, a, r1[:], start=True, stop=False),
                              reads=[Vb, rb], writes=[Ab])
                        kb.op("pe", lambda eng, o=A[0:N2, jj * 256:(jj + 1) * 256], a=V[:, :, 128 + j]: eng.matmul(o, a, r2[:], start=False, stop=True),
                              reads=[Vb, rb], writes=[Ab])
                    Av = A[0:N2, :].rearrange("p (j c) -> p j c", c=256)
                    kb.op("dve", lambda eng, o=P1[pi][:], a=Av: eng.tensor_tensor(out=o, in0=a, in1=tcd[:].unsqueeze(1).to_broadcast([N2, 2, 256]), op=ALU.mult),
                          reads=[Ab, tb_], writes=[P1b[pi]])
                    kb.op("dve", lambda eng, o=P2[pi][:], a=Av: eng.tensor_tensor(out=o, in0=a, in1=tsd[:].unsqueeze(1).to_broadcast([N2, 2, 256]), op=ALU.mult),
                          reads=[Ab, tb_], writes=[P2b[pi]])
                    kb.op("pool", lambda eng, o=APB[ai][:, hp * 2:hp * 2 + 2, 0:128], a=P1[pi][:, :, 0:128], b=P2[pi][:, :, 128:256]:
                          eng.tensor_tensor(out=o, in0=a, in1=b, op=ALU.subtract), reads=[P1b[pi], P2b[pi]], writes=[APBb[ai]])
                    kb.op("pool", lambda eng, o=APB[ai][:, hp * 2:hp * 2 + 2, 128:256], a=P1[pi][:, :, 128:256], b=P2[pi][:, :, 0:128]:
                          eng.tensor_tensor(out=o, in0=a, in1=b, op=ALU.add), reads=[P1b[pi], P2b[pi]], writes=[APBb[ai]])
                Fb, Fbb = bank()
                for c in range(4):
                    kb.op("pe", lambda eng, o=Fb[0:N2, c * 128:(c + 1) * 128], b=APB[ai][:, c, 0:128]: eng.matmul(o, c2[:], b, start=True, stop=False),
                          reads=[APBb[ai], cb_], writes=[Fbb])
                    kb.op("pe", lambda eng, o=Fb[0:N2, c * 128:(c + 1) * 128], b=APB[ai][:, c, 128:256]: eng.matmul(o, s2n[:], b, start=False, stop=True),
                          reads=[APBb[ai], cb_], writes=[Fbb])
                kb.op("act", lambda eng, o=Y[:, :, j4:j4 + 4].rearrange("p k c -> p c k"), a=Fb[0:N2, :].rearrange("p (c k) -> p c k", k=128):
                      eng.copy(out=o, in_=a), reads=[Fbb], writes=[Yb])
            st_ = max(1, N2 // 4)
            for h0 in range(0, N2, st_):
                kb.dma("pool", yv[h0:h0 + st_, :, 1536 + g * 128: 1536 + (g + 1) * 128], Y[h0:h0 + st_], src=Yb, dst=ydb)
        lc["end_phase"]()


    def s5():
        lc["new_phase"]("s5" + sname)
        C = lc["CST"]
        ident, identb = lc["ident"], lc["identb"]
        tbank, tbuf = lc["tbank"], lc["tbuf"]
        T = min(512, L)
        nch = L // T
        nsub = T // 128
        nblk = L // 128
        TWO_PI = 2.0 * math.pi
        iota1 = sb("s_iota", [128, 512], F32); maskh = sb("s_mh", [128, 4], F32); maskp = sb("s_mp", [128, 2], F32)
        maskpn = sb("s_mpn", [128, 2], F32); Jt = sb("s_J", [128, 128], BF16); pit = sb("s_pi", [128, 1], F32)
        cb = kb.buf("s_const")
        for t_, src in ((iota1, C["iota1"]), (maskh, C["maskh"]), (maskp, C["maskp"]), (maskpn, C["maskpn"]), (Jt, C["J"]), (pit, C["pi"])):
            kb.dma("sp", t_[:], src, dst=cb)
        Dt = sb("s_D", [128, 512], F32); Db = kb.buf("s_D")
        kb.dma("sp", Dt[:], P["s5_d"][l:l + 1, :].to_broadcast([128, 512]), dst=Db)
        Bbd = {}; Cbd = {}; RHO = {}; THM = {}
        pre = {}
        for d in range(2):
            pre["rho%d" % d] = sb("s_rho%d" % d, [128, 16], F32)
            pre["s1%d" % d] = sb("s_s1%d" % d, [128, 16], F32)
            pre["c1%d" % d] = sb("s_c1%d" % d, [128, 16], F32)
            for ri in range(2):
                pre["bbd%d%d" % (d, ri)] = sb("s_bbd%d%d" % (d, ri), [128, 4, 2, 2, 64], BF16)
                pre["cbd%d%d" % (d, ri)] = sb("s_cbd%d%d" % (d, ri), [128, 16, 2, 16], F32)
        pre["hpi"] = sb("s_hpi", [128, 1], F32)
        lc["push_scope"]()
        tmpn = [0]

        def tmp(shape):
            tmpn[0] += 1
            return sb("s_tmp%d" % tmpn[0], shape, F32), kb.buf("s_tmp%d" % tmpn[0])

        def tt(eng, out, a, b, op, reads, writes):
            kb.op(eng, lambda e, out=out, a=a, b=b, op=op: e.tensor_tensor(out=out, in0=a, in1=b, op=op), reads=reads, writes=writes)

        hpit = pre["hpi"]
        kb.op("pool", lambda e: e.memset(hpit[:], 0.5 * math.pi), writes=[cb])

        def sincos(src, srcb, shape, sn=None, snb=None, cs=None, csb=None):
            if sn is None:
                sn, snb = tmp(shape); cs, csb = tmp(shape)
            ta, tab_ = tmp(shape); tb2, tb2b = tmp(shape)
            kb.op("act", lambda e: e.activation(out=sn[:], in_=src, func=AF.Sin, scale=1.0 / 16.0), reads=[srcb], writes=[snb])
            kb.op("act", lambda e: e.activation(out=cs[:], in_=src, func=AF.Sin, bias=hpit[:], scale=-1.0 / 16.0), reads=[srcb, cb], writes=[csb])
            for _ in range(4):
                tt("dve", ta[:], cs[:], cs[:], ALU.mult, [csb], [tab_])
                tt("dve", tb2[:], sn[:], sn[:], ALU.mult, [snb], [tb2b])
                tt("dve", sn[:], sn[:], cs[:], ALU.mult, [snb, csb], [snb])
                kb.op("dve", lambda e: e.tensor_scalar(out=sn[:], in0=sn[:], scalar1=2.0, scalar2=None, op0=ALU.mult), reads=[snb], writes=[snb])
                tt("dve", cs[:], ta[:], tb2[:], ALU.subtract, [tab_, tb2b], [csb])
            return sn, snb, cs, csb

        for d in range(2):
            tmpn[0] = 100 * d
            LRs, LRsb = tmp([128, 16]); LIs, LIsb = tmp([128, 16]); LDs, LDsb = tmp([128, 16])
            kb.dma("sp", LRs[:], P["lam_re"][l, d].rearrange("(q g2) p -> (g2 p) q", g2=2), dst=LRsb)
            kb.dma("sp", LIs[:], P["lam_im"][l, d].rearrange("(q g2) p -> (g2 p) q", g2=2), dst=LIsb)
            ldv = P["log_dt"][l, d].rearrange("(q g2) -> g2 q", g2=2)
            for g2 in range(2):
                kb.dma("sp", LDs[g2 * 64:(g2 + 1) * 64, :], ldv[g2:g2 + 1, :].to_broadcast([64, 16]), dst=LDsb)
            DTs, DTsb = tmp([128, 16])
            kb.op("act", lambda e, o=DTs, a=LDs: e.activation(out=o[:], in_=a[:], func=AF.Exp), reads=[LDsb], writes=[DTsb])
            t0_, t0b = tmp([128, 16])
            tt("dve", t0_[:], LRs[:], DTs[:], ALU.mult, [LRsb, DTsb], [t0b])
            rho = pre["rho%d" % d]; rhob = kb.buf("s_rho")
            kb.op("act", lambda e, o=rho, a=t0_: e.activation(out=o[:], in_=a[:], func=AF.Exp), reads=[t0b], writes=[rhob])
            th, thb = tmp([128, 16])
            tt("dve", th[:], LIs[:], DTs[:], ALU.mult, [LIsb, DTsb], [thb])
            s1b = kb.buf("s_s1"); c1b = kb.buf("s_c1")
            sincos(th[:], thb, [128, 16], pre["s1%d" % d], s1b, pre["c1%d" % d], c1b)
            RHO[d] = (rho, rhob); THM[d] = (pre["s1%d" % d], s1b, pre["c1%d" % d], c1b)
            LRc, LRcb = tmp([128, 4, 64]); LIc, LIcb = tmp([128, 4, 64]); LDc, LDcb = tmp([128, 4])
            BRc, BRcb = tmp([128, 4, 64]); BIc, BIcb = tmp([128, 4, 64])
            lrv = P["lam_re"][l, d].rearrange("(ct pg) p -> pg ct p", pg=8)
            liv = P["lam_im"][l, d].rearrange("(ct pg) p -> pg ct p", pg=8)
            ldv2 = P["log_dt"][l, d].rearrange("(ct pg) -> pg ct", pg=8)
            brv = P["b_re"][l, d].rearrange("(ct pg) p h -> pg h ct p", pg=8)
            biv = P["b_im"][l, d].rearrange("(ct pg) p h -> pg h ct p", pg=8)
            for pg in range(8):
                sl = slice(pg * 16, (pg + 1) * 16)
                kb.dma("sp", LRc[sl], lrv[pg:pg + 1].to_broadcast([16, 4, 64]), dst=LRcb)
                kb.dma("sp", LIc[sl], liv[pg:pg + 1].to_broadcast([16, 4, 64]), dst=LIcb)
                kb.dma("sp", LDc[sl], ldv2[pg:pg + 1].to_broadcast([16, 4]), dst=LDcb)
                for ct_ in range(4):
                    kb.dma("sp", BRc[sl, ct_, :], brv[pg, :, ct_, :], dst=BRcb)
                    kb.dma("sp", BIc[sl, ct_, :], biv[pg, :, ct_, :], dst=BIcb)
            DTc, DTcb = tmp([128, 4])
            kb.op("act", lambda e, o=DTc, a=LDc: e.activation(out=o[:], in_=a[:], func=AF.Exp), reads=[LDcb], writes=[DTcb])
            dtb = DTc[:].unsqueeze(2).to_broadcast([128, 4, 64])
            lrdt, lrdtb = tmp([128, 4, 64])
            tt("dve", lrdt[:], LRc[:], dtb, ALU.mult, [LRcb, DTcb], [lrdtb])
            rhoc, rhocb = tmp([128, 4, 64])
            kb.op("act", lambda e, o=rhoc, a=lrdt: e.activation(out=o[:], in_=a[:], func=AF.Exp), reads=[lrdtb], writes=[rhocb])
            thc, thcb = tmp([128, 4, 64])
            tt("dve", thc[:], LIc[:], dtb, ALU.mult, [LIcb, DTcb], [thcb])
            sn, snb, cs, csb = sincos(thc[:], thcb, [128, 4, 64])
            ar, arb = tmp([128, 4, 64]); ai, aib = tmp([128, 4, 64])
            tt("dve", ar[:], rhoc[:], cs[:], ALU.mult, [rhocb, csb], [arb])
            tt("dve", ai[:], rhoc[:], sn[:], ALU.mult, [rhocb, snb], [aib])
            kb.op("dve", lambda e, a=ar: e.tensor_scalar(out=a[:], in0=a[:], scalar1=-1.0, scalar2=None, op0=ALU.add), reads=[arb], writes=[arb])
            den, denb = tmp([128, 4, 64]); t1, t1b = tmp([128, 4, 64])
            tt("dve", den[:], LRc[:], LRc[:], ALU.mult, [LRcb], [denb])
            tt("dve", t1[:], LIc[:], LIc[:], ALU.mult, [LIcb], [t1b])
            tt("dve", den[:], den[:], t1[:], ALU.add, [denb, t1b], [denb])
            kb.op("dve", lambda e, a=den: e.reciprocal(out=a[:], in_=a[:]), reads=[denb], writes=[denb])
            fr, frb = tmp([128, 4, 64]); fi, fib = tmp([128, 4, 64]); t2, t2b = tmp([128, 4, 64])
            tt("dve", fr[:], ar[:], LRc[:], ALU.mult, [arb, LRcb], [frb])
            tt("dve", t2[:], ai[:], LIc[:], ALU.mult, [aib, LIcb], [t2b])
            tt("dve", fr[:], fr[:], t2[:], ALU.add, [frb, t2b], [frb])
            tt("dve", fr[:], fr[:], den[:], ALU.mult, [frb, denb], [frb])
            tt("dve", fi[:], ai[:], LRc[:], ALU.mult, [aib, LRcb], [fib])
            tt("dve", t2[:], ar[:], LIc[:], ALU.mult, [arb, LIcb], [t2b])
            tt("dve", fi[:], fi[:], t2[:], ALU.subtract, [fib, t2b], [fib])
            tt("dve", fi[:], fi[:], den[:], ALU.mult, [fib, denb], [fib])
            bbr, bbrb = tmp([128, 4, 64]); bbi, bbib = tmp([128, 4, 64])
            tt("dve", bbr[:], fr[:], BRc[:], ALU.mult, [frb, BRcb], [bbrb])
            tt("dve", t2[:], fi[:], BIc[:], ALU.mult, [fib, BIcb], [t2b])
            tt("dve", bbr[:], bbr[:], t2[:], ALU.subtract, [bbrb, t2b], [bbrb])
            tt("dve", bbi[:], fr[:], BIc[:], ALU.mult, [frb, BIcb], [bbib])
            tt("dve", t2[:], fi[:], BRc[:], ALU.mult, [fib, BRcb], [t2b])
            tt("dve", bbi[:], bbi[:], t2[:], ALU.add, [bbib, t2b], [bbib])
            for ri, src_, srcb_ in ((0, bbr, bbrb), (1, bbi, bbib)):
                bd = pre["bbd%d%d" % (d, ri)]; bdb = kb.buf("s_bbd")
                for sg in range(4):
                    kb.op("dve", lambda e, o=bd[:, :, sg // 2, sg % 2, :], a=src_, sg=sg: e.tensor_scalar(out=o, in0=a[:], scalar1=maskh[:, sg:sg + 1], scalar2=None, op0=ALU.mult),
                          reads=[srcb_, cb], writes=[bdb])
                Bbd[(d, ri)] = (bd, bdb)
            for ri, key, mk in ((0, "c_re", maskp), (1, "c_im", maskpn)):
                cs_, csb_ = tmp([128, 16, 16])
                cv = P[key][l, d].rearrange("(q g2) ho p -> g2 p q ho", g2=2)
                for g2 in range(2):
                    for q_ in range(16):
                        kb.dma("sp", cs_[g2 * 64:(g2 + 1) * 64, q_, :], cv[g2, :, q_, :], dst=csb_)
                cd = pre["cbd%d%d" % (d, ri)]; cdb = kb.buf("s_cbd")
                for g2 in range(2):
                    kb.op("dve", lambda e, o=cd[:, :, g2, :], a=cs_, g2=g2, mk=mk: e.tensor_scalar(out=o, in0=a[:], scalar1=mk[:, g2:g2 + 1], scalar2=None, op0=ALU.mult),
                          reads=[csb_, cb], writes=[cdb])
                Cbd[(d, ri)] = (cd, cdb)
        lc["pop_scope"]()
        uT = sb("s_uT", [128, L], BF16); uTb = kb.buf("s_uT")
        uTr = sb("s_uTr", [128, L], BF16); uTrb = kb.buf("s_uTr")
        ub = [sb("s_ub%d" % i, [128, 4, 128], BF16) for i in range(2)]; ubb = [kb.buf("s_ub%d" % i) for i in range(2)]
        Yacc = [sb("s_Y%d" % d, [128, nblk, 128], BF16) for d in range(2)]
        Yaccb = [[kb.buf("s_Y%d_%d" % (d, c)) for c in range(nch)] for d in range(2)]
        wk2 = [{nm: (sb("s_w%d_" % j + nm, [128, T], F32), kb.buf("s_w%d_" % j + nm)) for nm in
                ("t1", "t2", "t3", "t4", "kr", "ki", "wr", "wi", "p1", "p2", "p3", "p4", "xr", "xi")} for j in range(1)]
        wkx = [{nm: (sb("s_x%d_" % j + nm, [128, T], F32), kb.buf("s_x%d_" % j + nm)) for nm in ("kr", "ki", "wr", "wi")} for j in range(1)]
        ctmp = sb("s_ctmp", [128, 2], F32); ctmpb = kb.buf("s_ctmp")
        chi = [0]
        sint = sb("s_sin", [128, T], F32); cost = sb("s_cos", [128, T], F32); m1t = sb("s_m1", [128, T], F32); m2t = sb("s_m2", [128, T], F32)
        sinb = kb.buf("s_sin"); cosb = kb.buf("s_cos"); m1b_ = kb.buf("s_m1"); m2b_ = kb.buf("s_m2")
        cry = sb("s_cry", [128, 2], F32); cryb = kb.buf("s_cry")
        assert T >= min(4, nblk) * 128
        fin = {"s2": wk2[0]["t1"], "x2": wk2[0]["t2"], "sg": wk2[0]["t3"], "du": wk2[0]["t4"]}
        yo = sb("s_yo", [128, 4, 128], BF16); yob = kb.buf("s_yo")
        nb4 = min(4, nblk)
        ui = 0
        for ct in range(4):
            for b0 in range(0, nblk, nb4):
                i = ui % 2
                ui += 1
                kb.dma("sp", ub[i][:, 0:nb4, :], zd[b0 * 128:(b0 + nb4) * 128, ct * 128:(ct + 1) * 128].rearrange("(b t) c -> t b c", t=128),
                       src=zdb, dst=ubb[i])
                for rev in range(2):
                    tbk, tbkb = tbank[rev], tbuf[rev]
                    for bb in range(nb4):
                        col = bb if rev == 0 else (nb4 - 1 - bb)
                        kb.op("pe", lambda e, o=tbk[:, col * 128:(col + 1) * 128], a=ub[i][:, bb, :], idm=(ident if rev == 0 else Jt):
                              e.transpose(out=o, in_=a, identity=idm[:]), reads=[ubb[i], identb, cb], writes=[tbkb])
                    if rev == 0:
                        kb.op("act", lambda e, o=uT[:, b0 * 128:(b0 + nb4) * 128], a=tbk[:, 0:nb4 * 128]: e.copy(out=o, in_=a), reads=[tbkb], writes=[uTb])
                    else:
                        r0 = L - (b0 + nb4) * 128
                        kb.op("act", lambda e, o=uTr[:, r0:r0 + nb4 * 128], a=tbk[:, 0:nb4 * 128]: e.copy(out=o, in_=a), reads=[tbkb], writes=[uTrb])
            for d in range(2):
                U, Ub = (uT, uTb) if d == 0 else (uTr, uTrb)
                rho, rhob = RHO[d]; s1, s1b, c1, c1b = THM[d]
                for q4 in range(4):
                    q = ct * 4 + q4
                    kb.op("act", lambda e, q=q, c1=c1: e.copy(out=cost[:, 0:1], in_=c1[:, q:q + 1]), reads=[c1b], writes=[cosb])
                    kb.op("act", lambda e, q=q, s1=s1: e.copy(out=sint[:, 0:1], in_=s1[:, q:q + 1]), reads=[s1b], writes=[sinb])
                    m_ = 1
                    while m_ < T:
                        kb.op("pool", lambda e, m_=m_: e.tensor_scalar(out=m1t[:, 0:m_], in0=sint[:, 0:m_], scalar1=sint[:, m_ - 1:m_], scalar2=None, op0=ALU.mult),
                              reads=[sinb], writes=[m1b_])
                        kb.op("pool", lambda e, m_=m_: e.tensor_scalar(out=m2t[:, 0:m_], in0=sint[:, 0:m_], scalar1=cost[:, m_ - 1:m_], scalar2=None, op0=ALU.mult),
                              reads=[sinb, cosb], writes=[m2b_])
                        kb.op("dve", lambda e, m_=m_: e.scalar_tensor_tensor(out=sint[:, m_:2 * m_], in0=cost[:, 0:m_], scalar=sint[:, m_ - 1:m_], in1=m2t[:, 0:m_],
                                                                           op0=ALU.mult, op1=ALU.add), reads=[cosb, sinb, m2b_], writes=[sinb])
                        kb.op("dve", lambda e, m_=m_: e.scalar_tensor_tensor(out=cost[:, m_:2 * m_], in0=cost[:, 0:m_], scalar=cost[:, m_ - 1:m_], in1=m1t[:, 0:m_],
                                                                           op0=ALU.mult, op1=ALU.subtract), reads=[cosb, m1b_], writes=[cosb])
                        m_ *= 2
                    kb.op("pool", lambda e: e.memset(cry[:], 0.0), writes=[cryb])
                    for c in range(nch):
                        pr, prb = bank(); pi_, pib = bank()
                        for ri, (pp_, ppb_) in ((0, (pr, prb)), (1, (pi_, pib))):
                            bd, bdb = Bbd[(d, ri)]
                            hs = slice((q4 // 2) * 64, (q4 // 2 + 1) * 64)
                            kb.op("pe", lambda e, o=pp_[:, 0:T], a=bd[hs, ct, q4 % 2, :, :].rearrange("p a b -> p (a b)"),
                                  b=U[hs, c * T:(c + 1) * T]: e.matmul(o, a, b, start=True, stop=True),
                                  reads=[bdb, Ub], writes=[ppb_])
                        wkc = dict(wk2[0])
                        if chi[0] % 2 == 1:
                            wkc.update(wkx[0])
                        chi[0] += 1
                        W_ = {k: v[0] for k, v in wkc.items()}; Wb_ = {k: v[1] for k, v in wkc.items()}
                        tt("dve", W_["t1"][:], pr[:, 0:T], cost[:], ALU.mult, [prb, cosb], [Wb_["t1"]])
                        tt("dve", W_["t2"][:], pi_[:, 0:T], sint[:], ALU.mult, [pib, sinb], [Wb_["t2"]])
                        tt("dve", W_["t3"][:], pi_[:, 0:T], cost[:], ALU.mult, [pib, cosb], [Wb_["t3"]])
                        tt("dve", W_["t4"][:], pr[:, 0:T], sint[:], ALU.mult, [prb, sinb], [Wb_["t4"]])
                        tt("pool", W_["kr"][:], W_["t1"][:], W_["t2"][:], ALU.add, [Wb_["t1"], Wb_["t2"]], [Wb_["kr"]])
                        tt("pool", W_["ki"][:], W_["t3"][:], W_["t4"][:], ALU.subtract, [Wb_["t3"], Wb_["t4"]], [Wb_["ki"]])
                        for nm_k, nm_w, cc in (("kr", "wr", 0), ("ki", "wi", 1)):
                            kb.op("dve", lambda e, o=W_[nm_w], k_=W_[nm_k], cc=cc, q=q, rho=rho: e.tensor_tensor_scan(
                                out=o[:], data0=rho[:, q:q + 1].to_broadcast([128, T]), data1=k_[:], initial=cry[:, cc:cc + 1],
                                op0=ALU.mult, op1=ALU.add), reads=[Wb_[nm_k], rhob, cryb], writes=[Wb_[nm_w]])
                        tt("dve", ctmp[:, 0:1], W_["wi"][:, T - 1:T], sint[:, T - 1:T], ALU.mult, [Wb_["wi"], sinb], [ctmpb])
                        tt("dve", ctmp[:, 1:2], W_["wr"][:, T - 1:T], sint[:, T - 1:T], ALU.mult, [Wb_["wr"], sinb], [ctmpb])
                        kb.op("dve", lambda e, wr_=W_["wr"]: e.scalar_tensor_tensor(out=cry[:, 0:1], in0=wr_[:, T - 1:T], scalar=cost[:, T - 1:T], in1=ctmp[:, 0:1],
                                                                               op0=ALU.mult, op1=ALU.subtract), reads=[Wb_["wr"], cosb, ctmpb], writes=[cryb])
                        kb.op("dve", lambda e, wi_=W_["wi"]: e.scalar_tensor_tensor(out=cry[:, 1:2], in0=wi_[:, T - 1:T], scalar=cost[:, T - 1:T], in1=ctmp[:, 1:2],
                                                                               op0=ALU.mult, op1=ALU.add), reads=[Wb_["wi"], cosb, ctmpb], writes=[cryb])
                        tt("pool", W_["p1"][:], W_["wr"][:], cost[:], ALU.mult, [Wb_["wr"], cosb], [Wb_["p1"]])
                        tt("pool", W_["p2"][:], W_["wi"][:], sint[:], ALU.mult, [Wb_["wi"], sinb], [Wb_["p2"]])
                        tt("pool", W_["p3"][:], W_["wi"][:], cost[:], ALU.mult, [Wb_["wi"], cosb], [Wb_["p3"]])
                        tt("pool", W_["p4"][:], W_["wr"][:], sint[:], ALU.mult, [Wb_["wr"], sinb], [Wb_["p4"]])
                        tt("dve", W_["xr"][:], W_["p1"][:], W_["p2"][:], ALU.subtract, [Wb_["p1"], Wb_["p2"]], [Wb_["xr"]])
                        tt("dve", W_["xi"][:], W_["p3"][:], W_["p4"][:], ALU.add, [Wb_["p3"], Wb_["p4"]], [Wb_["xi"]])
                        py, pyb = bank()
                        for sub in range(nsub):
                            for ri, nm in ((0, "xr"), (1, "xi")):
                                cd, cdb = Cbd[(d, ri)]
                                kb.op("pe", lambda e, o=py[:, sub * 32:(sub + 1) * 32], a=W_[nm][:, sub * 128:(sub + 1) * 128],
                                      b=cd[:, q, :, :].rearrange("p a b -> p (a b)"), ri=ri: e.matmul(o, a, b, start=(ri == 0), stop=(ri == 1)),
                                      reads=[Wb_[nm], cdb], writes=[pyb])
                        kb.op("act", lambda e, o=Yacc[d][:, c * nsub:(c + 1) * nsub, q4 * 32:(q4 + 1) * 32],
                              a=py[:, 0:nsub * 32].rearrange("p (s c) -> p s c", c=32): e.copy(out=o, in_=a), reads=[pyb], writes=[Yaccb[d][c]])
            for b0 in range(0, nblk, nb4):
                i = ui % 2
                ui += 1
                kb.dma("sp", ub[i][:, 0:nb4, :], zd[b0 * 128:(b0 + nb4) * 128, ct * 128:(ct + 1) * 128].rearrange("(b t) c -> t b c", t=128),
                       src=zdb, dst=ubb[i])
                pf, pfb = bank()
                for bb in range(nb4):
                    b = b0 + bb
                    rb_ = nblk - 1 - b
                    kb.op("pe", lambda e, o=pf[:, bb * 128:(bb + 1) * 128], b_=Yacc[1][:, rb_, :]: e.matmul(o, Jt[:], b_, start=True, stop=False),
                          reads=[cb, Yaccb[1][rb_ // nsub]], writes=[pfb])
                    kb.op("pe", lambda e, o=pf[:, bb * 128:(bb + 1) * 128], b_=Yacc[0][:, b, :]: e.matmul(o, ident[:], b_, start=False, stop=True),
                          reads=[identb, Yaccb[0][b // nsub]], writes=[pfb])
                n_ = nb4 * 128
                S2, S2b = fin["s2"]; X2, X2b = fin["x2"]; SG, SGb = fin["sg"]; DU, DUb = fin["du"]
                kb.op("pool", lambda e, o=DU[:, 0:n_].rearrange("p (b c) -> p b c", c=128), a=ub[i][:, 0:nb4, :],
                      dd=Dt[:, ct * 128:(ct + 1) * 128].unsqueeze(1).to_broadcast([128, nb4, 128]): e.tensor_tensor(out=o, in0=a, in1=dd, op=ALU.mult),
                      reads=[ubb[i], Db], writes=[DUb])
                tt("dve", S2[:, 0:n_], pf[:, 0:n_], DU[:, 0:n_], ALU.add, [pfb, DUb], [S2b])
                kb.op("act", lambda e: e.activation(out=X2[:, 0:n_], in_=S2[:, 0:n_], func=AF.Square), reads=[S2b], writes=[X2b])
                kb.op("dve", lambda e: e.tensor_scalar(out=X2[:, 0:n_], in0=X2[:, 0:n_], scalar1=0.044715, scalar2=1.0, op0=ALU.mult, op1=ALU.add),
                      reads=[X2b], writes=[X2b])
                tt("pool", X2[:, 0:n_], X2[:, 0:n_], S2[:, 0:n_], ALU.mult, [X2b, S2b], [X2b])
                kb.op("act", lambda e: e.activation(out=SG[:, 0:n_], in_=X2[:, 0:n_], func=AF.Sigmoid, scale=2.0 * math.sqrt(2.0 / math.pi)),
                      reads=[X2b], writes=[SGb])
                tt("dve", yo[:, 0:nb4, :].rearrange("p b c -> p (b c)"), S2[:, 0:n_], SG[:, 0:n_], ALU.mult, [S2b, SGb], [yob])
                kb.dma("pool", yd[b0 * 128:(b0 + nb4) * 128, ct * 128:(ct + 1) * 128].rearrange("(b t) c -> t b c", t=128), yo[:, 0:nb4, :],
                       src=yob, dst=ydb)
        lc["end_phase"]()


    def na():
        lc["new_phase"]("na" + sname)
        C = lc["CST"]
        ident, identb = lc["ident"], lc["identb"]
        tbank, tbuf = lc["tbank"], lc["tbuf"]
        R = L // 64
        nblk = L // 128
        oh2 = sb("n_oh2", [62, 64, 128], BF16); mask2 = sb("n_mask2", [128, 64], F32); cb = kb.buf("n_const")
        kb.dma("sp", oh2[:], C["oh2"], dst=cb)
        kb.dma("sp", mask2[:], C["mask2"], dst=cb)
        RB = sb("n_rb", [62, 8, 14], F32); RBb = kb.buf("n_rb")
        for a in range(2):
            for h in range(8):
                kb.dma("sp", RB[a * 31:(a + 1) * 31, h, :], P["rel_bias"][l, h, a:a + 14, :].rearrange("dr m -> m dr"), dst=RBb)
        RBh = sb("n_rbh", [62, 8, 14], BF16); RBhb = kb.buf("n_rbh")
        kb.op("dve", lambda e: e.tensor_copy(out=RBh[:], in_=RB[:]), reads=[RBb], writes=[RBhb])
        BT = sb("n_bt", [128, 8, 14, 64], BF16); BTb = kb.buf("n_bt")
        for h in range(8):
            for half in range(2):
                bp, bpb = bank()
                for ql in range(32):
                    qc = half * 32 + ql
                    kb.op("pe", lambda e, o=bp[:, ql * 14:(ql + 1) * 14], a=oh2[:, qc, :], b=RBh[:, h, :]: e.matmul(o, a, b, start=True, stop=True),
                          reads=[cb, RBhb], writes=[bpb])
                kb.op("dve", lambda e, o=BT[:, h, :, half * 32:(half + 1) * 32], a=bp[:, 0:448].rearrange("p (q d) -> p d q", d=14),
                      mk=mask2[:, half * 32:(half + 1) * 32].unsqueeze(1).to_broadcast([128, 14, 32]): e.tensor_tensor(out=o, in0=a, in1=mk, op=ALU.add),
                      reads=[bpb, cb], writes=[BTb])
        QT = sb("n_QT", [128, L], BF16); QTb = kb.buf("n_QT")
        KT = sb("n_KT", [128, L], BF16); KTb = kb.buf("n_KT")
        Va = [sb("n_Va%d" % a, [128, nblk, 2, 65], BF16) for a in range(2)]; Vab = [kb.buf("n_Va%d" % a) for a in range(2)]
        ld = [sb("n_ld%d" % i, [128, 4, 128], BF16) for i in range(2)]; ldb = [kb.buf("n_ld%d" % i) for i in range(2)]
        PT = [sb("n_PT%d" % i, [128, 2, 4, 64], BF16) for i in range(2)]; PTb = [kb.buf("n_PT%d" % i) for i in range(2)]
        RF = 64 if R >= 64 else R
        Yst = sb("n_Y", [64, RF, 128], BF16); Ystb = kb.buf("n_Y")
        rec = sb("n_rec", [64, 2, 1], F32); recb = kb.buf("n_rec")
        nb4 = min(4, nblk)
        li = 0
        ti = 0
        pti = 0
        ydv = yd.rearrange("(r q) c -> q r c", q=64)
        for hp in range(4):
            for a in range(2):
                kb.op("pool", lambda e, a=a: e.memset(Va[a][:, :, :, 64:65], 1.0), writes=[Vab[a]])
            vcol = slice(3072 + hp * 128, 3072 + (hp + 1) * 128)
            for hh in range(2):
                vc0 = 3072 + hp * 128 + hh * 64
                bs_ = 32
                for b0 in range(0, nblk, bs_):
                    b1 = min(nblk, b0 + bs_)
                    kb.dma("sp", Va[0][:, b0:b1, hh, 0:64], zd[b0 * 128:b1 * 128, vc0:vc0 + 64].rearrange("(b t) d -> t b d", t=128), src=zdb, dst=Vab[0])
                    b1 = min(nblk - 1, b0 + bs_)
                    if b1 > b0:
                        kb.dma("sp", Va[1][:, b0:b1, hh, 0:64], zd[64 + b0 * 128:64 + b1 * 128, vc0:vc0 + 64].rearrange("(b t) d -> t b d", t=128), src=zdb, dst=Vab[1])
            for (dst, dstb, c0) in ((QT, QTb, 2048 + hp * 128), (KT, KTb, 2560 + hp * 128)):
                for b0 in range(0, nblk, nb4):
                    i = li % 2
                    li += 1
                    kb.dma("sp", ld[i][:, 0:nb4, :], zd[b0 * 128:(b0 + nb4) * 128, c0:c0 + 128].rearrange("(b t) c -> t b c", t=128), src=zdb, dst=ldb[i])
                    tbk, tbkb = tbank[ti % 2], tbuf[ti % 2]
                    ti += 1
                    for bb in range(nb4):
                        kb.op("pe", lambda e, o=tbk[:, bb * 128:(bb + 1) * 128], a=ld[i][:, bb, :]: e.transpose(out=o, in_=a, identity=ident[:]),
                              reads=[ldb[i], identb], writes=[tbkb])
                    kb.op("act", lambda e, o=dst[:, b0 * 128:(b0 + nb4) * 128], a=tbk[:, 0:nb4 * 128]: e.copy(out=o, in_=a), reads=[tbkb], writes=[dstb])
            for r in range(R):
                rs = min(max(r - 4, 0), R - 8)
                S, Sb = bank()
                Sv = S[:].rearrange("p (h b q) -> p h b q", h=2, b=4)
                for hh in range(2):
                    h = hp * 2 + hh
                    ps_ = slice(hh * 64, (hh + 1) * 64)
                    for b in range(4):
                        a0 = rs + 2 * b
                        dr0 = a0 - r + 7
                        kb.op("pe", lambda e, o=Sv[:, hh, b, :], a=KT[ps_, a0 * 64:a0 * 64 + 128], q_=QT[ps_, r * 64:(r + 1) * 64]:
                              e.matmul(o, a, q_, start=True, stop=False), reads=[KTb, QTb], writes=[Sb])
                        kb.op("pe", lambda e, o=Sv[:, hh, b, :], bt=BT[:, h, dr0, :]: e.matmul(o, ident[:], bt, start=False, stop=True),
                              reads=[identb, BTb], writes=[Sb])
                pi_ = pti % 2
                pti += 1
                kb.op("act", lambda e, o=PT[pi_][:].rearrange("p h b q -> p (h b q)"), a=S[:]: e.activation(out=o, in_=a, func=AF.Exp),
                      reads=[Sb], writes=[PTb[pi_]])
                O, Ob = bank()
                Ov = O[0:64, 0:130].rearrange("p (h c) -> p h c", c=65)
                for hh in range(2):
                    for b in range(4):
                        a0 = rs + 2 * b
                        al = a0 % 2
                        tix = a0 // 2
                        kb.op("pe", lambda e, o=Ov[:, hh, :], a=PT[pi_][:, hh, b, :], v=Va[al][:, tix, hh, :], b=b: e.matmul(o, a, v, start=(b == 0), stop=(b == 3)),
                              reads=[PTb[pi_], Vab[al]], writes=[Ob])
                kb.op("dve", lambda e, a=Ov[:, :, 64:65]: e.reciprocal(out=rec[:], in_=a), reads=[Ob], writes=[recb])
                rl = r % RF
                kb.op("dve", lambda e, o=Yst[:, rl, :].rearrange("p (h d) -> p h d", d=64), a=Ov[:, :, 0:64]:
                      e.tensor_tensor(out=o, in0=a, in1=rec[:].to_broadcast([64, 2, 64]), op=ALU.mult), reads=[Ob, recb], writes=[Ystb])
                if rl == RF - 1:
                    r0 = r - (RF - 1)
                    kb.dma("pool", ydv[:, r0:r0 + RF, 1024 + hp * 128:1024 + (hp + 1) * 128], Yst[:], src=Ystb, dst=ydb)
        lc["end_phase"]()

    if MIX_ENABLE["s5"]:
        s5()
    else:
        zero_cols(0, 512)
    if MIX_ENABLE["conv"]:
        conv()
    else:
        zero_cols(512, 1024)
    if MIX_ENABLE["na"]:
        na()
    else:
        zero_cols(1024, 1536)
    if MIX_ENABLE["fnet"]:
        fnet()
    else:
        zero_cols(1536, 2048)


class Buf:
    __slots__ = ("name", "w", "r", "sem", "cnt")

    def __init__(self, name):
        self.name = name
        self.w = None
        self.r = {}
        self.sem = None
        self.cnt = 0


class KB:
    def __init__(self, nc, es):
        self.nc = nc
        self.es = es
        self.eng = {"pe": nc.tensor, "act": nc.scalar, "dve": nc.vector, "pool": nc.gpsimd, "sp": nc.sync}
        self.lists = {k: [] for k in self.eng}
        self.sem = {k: es.enter_context(nc.semaphore("e_" + k)) for k in ("pe", "act", "dve", "pool")}
        self.cnt = {k: 0 for k in self.sem}
        self.seen = {k: {} for k in self.eng}
        self.dsems = {}
        self.owners = {}
        self.nb = 0

    def buf(self, name=None):
        self.nb += 1
        return Buf(name or f"b{self.nb}")

    def _dsem(self, b):
        if b.sem is None:
            nm = b.name
            own = self.owners.get(nm)
            if own is None:
                own = Buf("own_" + nm)
                own.sem = self.es.enter_context(self.nc.semaphore("d_%d_%s" % (len(self.owners), nm[:8])))
                self.owners[nm] = own
                self.dsems[id(own)] = own
            b.sem = own
        return b.sem

    def _waits(self, e, reads, writes):
        ev = []
        for b in reads:
            if b is not None and b.w is not None:
                ev.append(b.w)
        for b in writes:
            if b is None:
                continue
            if b.w is not None:
                ev.append(b.w)
            ev.extend(b.r.values())
        out = {}
        for x in ev:
            if x[0] == "c":
                _, pe, n = x
                if pe == e and e == "pe":
                    continue
                key = ("c", pe)
                sem = self.sem[pe]
                val = n
            else:
                _, owner = x
                key = ("d", id(owner))
                sem = owner.sem
                val = owner.cnt
            if self.seen[e].get(key, 0) >= val:
                continue
            if key not in out or out[key][1] < val:
                out[key] = (sem, val)
        for key, (sem, val) in out.items():
            self.seen[e][key] = val
            self.lists[e].append(("w", sem, val))

    def op(self, e, fn, reads=(), writes=()):
        self._waits(e, reads, writes)
        self.cnt[e] += 1
        n = self.cnt[e]
        self.lists[e].append(("i", fn, self.sem[e], 1))
        evt = ("c", e, n)
        for b in reads:
            if b is not None:
                b.r[("c", e)] = evt
        for b in writes:
            if b is not None:
                b.w = evt
                b.r = {}

    def dma(self, q, out, in_, src=None, dst=None):
        self._waits(q, [src], [dst])
        holder = dst if dst is not None else src
        owner = self._dsem(holder)
        owner.cnt += 16
        self.lists[q].append(("i", lambda eng, o=out, i=in_: eng.dma_start(out=o, in_=i), owner.sem, 16))
        evt = ("d", owner)
        if src is not None:
            src.r[("d", id(owner))] = evt
        if dst is not None:
            dst.w = evt
            dst.r = {}

    def barrier(self):
        for e in self.eng:
            for e2 in self.sem:
                if e2 == e:
                    continue
                key = ("c", e2)
                val = self.cnt[e2]
                if val > self.seen[e].get(key, 0):
                    self.seen[e][key] = val
                    self.lists[e].append(("w", self.sem[e2], val))
            for owner in self.dsems.values():
                key = ("d", id(owner))
                if owner.cnt > self.seen[e].get(key, 0):
                    self.seen[e][key] = owner.cnt
                    self.lists[e].append(("w", owner.sem, owner.cnt))

    def finish_wait(self, q, bufs):
        self._waits(q, bufs, bufs)

    def replay(self, block):
        def mk(name):
            lst = self.lists[name]

            def run(eng):
                for it in lst:
                    if it[0] == "w":
                        eng.wait_ge(it[1], it[2])
                    else:
                        it[1](eng).then_inc(it[2], it[3])
            return run
        block.tensor(mk("pe"))
        block.scalar(mk("act"))
        block.vector(mk("dve"))
        block.gpsimd(mk("pool"))
        block.sync(mk("sp"))


def _consts(L):
    N1 = 128
    N2 = L // 128
    n = np.arange(128)
    c = {}
    ang = 2 * np.pi * np.outer(n, n) / 128.0
    c["C1"], c["S1"] = np.cos(ang), np.sin(ang)
    n2 = np.arange(N2)
    ang = 2 * np.pi * np.outer(n2, n) / L
    c["Tc"], c["Ts"] = np.cos(ang), np.sin(ang)
    ang = 2 * np.pi * np.outer(n2, n2) / N2
    sc = 1.0 / math.sqrt(L * 128.0)
    c["C2"], c["S2n"] = np.cos(ang) * sc, -np.sin(ang) * sc
    return c


def build_program(LP, LS):
    nc = bass.Bass("TRN2", target_bir_lowering=False)
    seqs = [("p", LP), ("s", LS)]

    def din(name, shape, dt=F32):
        return nc.dram_tensor(name, list(shape), dt, kind="ExternalInput").ap()

    xin = {s: din("x_" + s, [L, D]) for s, L in seqs}
    pin = {s: din("p_" + s, [DEPTH, L, PLE]) for s, L in seqs}
    yout = {s: nc.dram_tensor("y_" + s, [L, D], F32, kind="ExternalOutput").ap() for s, L in seqs}
    g_mix = din("g_mix", [DEPTH, D]); g_ffn = din("g_ffn", [DEPTH, D]); g_ple = din("g_ple", [DEPTH, D])
    w_in = din("w_in", [DEPTH, D, MIX + 4 * D])
    lam_re = din("s5_lam_re", [DEPTH, 2, 32, 64]); lam_im = din("s5_lam_im", [DEPTH, 2, 32, 64])
    log_dt = din("s5_log_dt", [DEPTH, 2, 32])
    b_re = din("s5_b_re", [DEPTH, 2, 32, 64, 16]); b_im = din("s5_b_im", [DEPTH, 2, 32, 64, 16])
    c_re = din("s5_c_re", [DEPTH, 2, 32, 16, 64]); c_im = din("s5_c_im", [DEPTH, 2, 32, 16, 64])
    s5_d = din("s5_d", [DEPTH, 512]); w_glu = din("w_glu", [DEPTH, 512, 512]); conv_w = din("conv_w", [DEPTH, 3, 512])
    q_gain = din("q_gain", [DEPTH, 64]); k_gain = din("k_gain", [DEPTH, 64]); rel_bias = din("rel_bias", [DEPTH, 8, 15, 31])
    w_br = din("w_br", [DEPTH, 4, 512, D]); w_o = din("w_o", [DEPTH, D, D])
    w_ffn_in = din("w_ffn_in", [DEPTH, D, 2 * DFF]); w_ffn_out = din("w_ffn_out", [DEPTH, DFF, D])
    w_pg = din("w_ple_gate", [DEPTH, D, D]); w_pp = din("w_ple_proj", [DEPTH, PLE, D])
    PRM = dict(lam_re=lam_re, lam_im=lam_im, log_dt=log_dt, b_re=b_re, b_im=b_im, c_re=c_re, c_im=c_im, s5_d=s5_d,
               conv_w=conv_w, rel_bias=rel_bias)
    ident_d = din("c_ident", [128, 128], BF16)
    cs128_d = din("c_cs128", [128, 256], BF16)
    CST = dict(iota1=din("c_iota1", [128, 512]), maskh=din("c_maskh", [128, 4]), maskp=din("c_maskp", [128, 2]),
               maskpn=din("c_maskpn", [128, 2]), J=din("c_J", [128, 128], BF16), pi=din("c_pi", [128, 1]),
               oh2=din("c_oh2", [62, 64, 128], BF16), mask2=din("c_mask2", [128, 64]))
    ft = {}
    r1_d = din("c_r1", [128, 256], BF16); r2_d = din("c_r2", [128, 256], BF16)
    for s, L in seqs:
        N2 = L // 128
        ft[s] = dict(r1=r1_d, r2=r2_d,
                     tc=din("c_tc_" + s, [N2, 128]), ts=din("c_ts_" + s, [N2, 128]),
                     c2=din("c_c2_" + s, [N2, N2], BF16), s2n=din("c_s2n_" + s, [N2, N2], BF16))

    def dscr(name, shape, dt=BF16):
        return nc.dram_tensor(name, list(shape), dt).ap()

    W = {}
    for l in range(DEPTH):
        W[l] = dict(
            mix=dscr(f"wmix{l}", [8, 128, 16, 512]),
            gate=dscr(f"wgate{l}", [64, 128, 16, 128]),
            br=dscr(f"wbr{l}", [4, 16, 128, 4, 128]),
            glu=dscr(f"wglu{l}", [4, 128, 4, 128]),
            o=dscr(f"wo{l}", [4, 128, 16, 512]),
            f1=dscr(f"wf1{l}", [88, 128, 16, 128]),
            f2=dscr(f"wf2{l}", [4, 128, 44, 512]),
            pg=dscr(f"wpg{l}", [4, 128, 16, 512]),
            pp=dscr(f"wpp{l}", [4, 128, 2, 512]),
        )
    zbuf = {s: dscr("z_" + s, [L, ZC]) for s, L in seqs}
    ybuf = {s: dscr("ym_" + s, [L, D]) for s, L in seqs}
    xmid = {s: dscr("xm_" + s, [L, D], F32) for s, L in seqs}

    with ExitStack() as es:
        kb = KB(nc, es)

        cur = [es]

        nsb = [0]

        sbcache = {}

        def sb(name, shape, dt):
            key = (curkey[0], name, tuple(shape), str(dt))
            if key in sbcache:
                return sbcache[key]
            nsb[0] += 1
            t_ = cur[0].enter_context(nc.sbuf_tensor("%s_%d" % (name, nsb[0]), list(shape), dt))
            sbcache[key] = t_
            return t_

        curkey = [None]

        def new_phase(key=None):
            curkey[0] = key
            kb.barrier()
            if cur[0] is not es:
                cur[0].close()
            cur[0] = ExitStack()

        scopes = []

        def push_scope():
            scopes.append(cur[0])
            cur[0] = ExitStack()

        def pop_scope():
            kb.barrier()
            cur[0].close()
            cur[0] = scopes.pop()

        def end_phase():
            curkey[0] = None
            kb.barrier()
            if cur[0] is not es:
                cur[0].close()
            cur[0] = es

        def ps(name, shape, dt):
            return es.enter_context(nc.psum_tensor(name, list(shape), dt))

        pbank = [ps(f"pb{i}", [128, 512], F32) for i in range(6)]
        pbuf = [kb.buf(f"pb{i}") for i in range(6)]
        tbank = [ps(f"tb{i}", [128, 1024], BF16) for i in range(2)]
        tbuf = [kb.buf(f"tb{i}") for i in range(2)]
        prr = [0]

        def next_bank():
            i = prr[0] % 6
            prr[0] += 1
            return pbank[i], pbuf[i]

        trr = [0]

        def next_tbank():
            i = trr[0] % 2
            trr[0] += 1
            return tbank[i], tbuf[i]

        ident = sb("ident", [128, 128], BF16); identb = kb.buf("ident")
        kb.dma("sp", ident[:], ident_d, dst=identb)
        cs128 = sb("cs128", [128, 256], BF16); cs128b = kb.buf("cs128")
        kb.dma("sp", cs128[:], cs128_d, dst=cs128b)
        gv = sb("gvec", [128, 3 * DEPTH, 16], F32); gvb = kb.buf("gvec")
        qg = sb("qg", [128, DEPTH, 2, 512], F32); qgb = kb.buf("qg")
        new_phase()

        cv_in = [sb(f"cvin{i}", [128, 1024], F32) for i in range(2)]
        cv_inb = [kb.buf(f"cvin{i}") for i in range(2)]
        cv_out = [sb(f"cvout{i}", [128, 1024], BF16) for i in range(2)]
        cv_outb = [kb.buf(f"cvout{i}") for i in range(2)]
        cvi = [0]

        def convert(src, dst, nk, ncol):
            kstep = max(1, 1024 // ncol)
            for k0 in range(0, nk, kstep):
                k1 = min(nk, k0 + kstep)
                i = cvi[0] % 2
                cvi[0] += 1
                n = (k1 - k0) * ncol
                iv = cv_in[i][:, 0:n].rearrange("p (k c) -> p k c", c=ncol)
                ov = cv_out[i][:, 0:n].rearrange("p (k c) -> p k c", c=ncol)
                kb.dma("sp", iv, src[:, k0:k1, :], dst=cv_inb[i])
                e = "act" if (cvi[0] % 2) else "dve"
                if e == "act":
                    kb.op("act", lambda eng, o=cv_out[i][:, 0:n], a=cv_in[i][:, 0:n]: eng.copy(out=o, in_=a),
                          reads=[cv_inb[i]], writes=[cv_outb[i]])
                else:
                    kb.op("dve", lambda eng, o=cv_out[i][:, 0:n], a=cv_in[i][:, 0:n]: eng.tensor_copy(out=o, in_=a),
                          reads=[cv_inb[i]], writes=[cv_outb[i]])
                kb.dma("pool", dst[:, k0:k1, :], ov, src=cv_outb[i], dst=wscr)

        wscr = kb.buf("wscr")
        for l in range(DEPTH):
            win_v = w_in[l].rearrange("(k p) n -> p k n", p=128)
            for cb in range(8):
                convert(win_v[:, :, cb * 512:(cb + 1) * 512], W[l]["mix"][cb], 16, 512)
            for ct in range(64):
                convert(win_v[:, :, MIX + ct * 128: MIX + (ct + 1) * 128], W[l]["gate"][ct], 16, 128)
            for kbi in range(4):
                brv = w_br[l, kbi].rearrange("(k p) n -> p k n", p=128)
                for ct in range(16):
                    convert(brv[:, :, ct * 128:(ct + 1) * 128], W[l]["br"][kbi, ct], 4, 128)
            gluv = w_glu[l].rearrange("(k p) n -> p k n", p=128)
            for ct in range(4):
                convert(gluv[:, :, ct * 128:(ct + 1) * 128], W[l]["glu"][ct], 4, 128)
            ov_ = w_o[l].rearrange("(k p) n -> p k n", p=128)
            for cb in range(4):
                convert(ov_[:, :, cb * 512:(cb + 1) * 512], W[l]["o"][cb], 16, 512)
            f1v = w_ffn_in[l].rearrange("(k p) n -> p k n", p=128)
            for ct in range(88):
                convert(f1v[:, :, ct * 128:(ct + 1) * 128], W[l]["f1"][ct], 16, 128)
            f2v = w_ffn_out[l].rearrange("(k p) n -> p k n", p=128)
            for cb in range(4):
                convert(f2v[:, :, cb * 512:(cb + 1) * 512], W[l]["f2"][cb], 44, 512)
            pgv = w_pg[l].rearrange("(k p) n -> p k n", p=128)
            for cb in range(4):
                convert(pgv[:, :, cb * 512:(cb + 1) * 512], W[l]["pg"][cb], 16, 512)
            ppv = w_pp[l].rearrange("(k p) n -> p k n", p=128)
            for cb in range(4):
                convert(ppv[:, :, cb * 512:(cb + 1) * 512], W[l]["pp"][cb], 2, 512)

        kb.barrier()
        new_phase()
        xt = sb("xt", [128, 4, D], F32); xtb = [kb.buf(f"xt{i}") for i in range(4)]
        hb = [sb(f"h{i}", [128, D], BF16) for i in range(2)]; hbb = [kb.buf(f"h{i}") for i in range(2)]
        hT = sb("hT", [128, 16, TT], BF16); hTb = [kb.buf(f"hT{k}") for k in range(16)]
        for l in range(DEPTH):
            for j, g in enumerate((g_mix, g_ffn, g_ple)):
                kb.dma("sp", gv[:, l * 3 + j, :], g[l].rearrange("(k p) -> p k", p=128), dst=gvb)
        sq = sb("sqj", [128, D], BF16); sqb = kb.buf("sqj")
        st = sb("stat", [128, 8], F32); stb = kb.buf("stat")
        NWB = 4
        wb = [sb(f"wb{i}", [128, 11 * 512], BF16) for i in range(NWB)]; wbb = [kb.buf(f"wb{i}") for i in range(NWB)]
        wrr = [0]

        def load_w(src_ap, nk, ncol):
            i = wrr[0] % NWB
            wrr[0] += 1
            assert nk * ncol <= 11 * 512
            v = wb[i][:, 0:nk * ncol].rearrange("p (k c) -> p k c", c=ncol)
            kb.dma("sp", v, src_ap, src=wscr, dst=wbb[i])
            return v, wbb[i]

        def norm_T(gidx, nsub=4):
            for s_ in range(nsub):
                i = s_ % 2
                kb.op("act", lambda eng, a=xt[:, s_, :]: eng.activation(out=sq[:], in_=a, func=AF.Square,
                                                                       accum_out=st[:, 0:1]),
                      reads=[xtb[s_]], writes=[sqb, stb])
                kb.op("dve", lambda eng: eng.tensor_scalar(out=st[:, 1:2], in0=st[:, 0:1], scalar1=1.0 / D,
                                                           scalar2=EPS, op0=ALU.mult, op1=ALU.add),
                      reads=[stb], writes=[stb])
                kb.op("act", lambda eng: eng.sqrt(out=st[:, 3:4], in_=st[:, 1:2]), reads=[stb], writes=[stb])
                kb.op("dve", lambda eng: eng.reciprocal(out=st[:, 2:3], in_=st[:, 3:4]), reads=[stb], writes=[stb])
                kb.op("dve", lambda eng, a=xt[:, s_, :], o=hb[i][:]: eng.tensor_scalar(
                    out=o, in0=a, scalar1=st[:, 2:3], scalar2=None, op0=ALU.mult),
                    reads=[xtb[s_], stb], writes=[hbb[i]])
                for k4 in range(4):
                    tb_, tbb_ = next_tbank()
                    for kk in range(4):
                        k = k4 * 4 + kk
                        kb.op("pe", lambda eng, o=tb_[:, kk * 128:(kk + 1) * 128], a=hb[i][:, k * 128:(k + 1) * 128]:
                              eng.transpose(out=o, in_=a, identity=ident[:]),
                              reads=[hbb[i], identb], writes=[tbb_])
                    for kk in range(4):
                        k = k4 * 4 + kk
                        kb.op("act", lambda eng, o=hT[:, k, s_ * 128:(s_ + 1) * 128], a=tb_[:, kk * 128:(kk + 1) * 128],
                              sc=gv[:, gidx, k:k + 1]: eng.activation(out=o, in_=a, func=AF.Copy, scale=sc),
                              reads=[tbb_, gvb], writes=[hTb[k]])

        outb = kb.buf("outb")
        xmb = {s: kb.buf("xm_" + s) for s, _ in seqs}
        zt = [sb(f"zt{i}", [128, 512], BF16) for i in range(2)]; ztb = [kb.buf(f"zt{i}") for i in range(2)]
        zrr = [0]
        for l in range(DEPTH):
            for j, g in enumerate((q_gain, k_gain)):
                for hh in range(8):
                    kb.dma("sp", qg[:, l, j, hh * 64:(hh + 1) * 64], g[l:l + 1, :].partition_broadcast(128) if False else g[l:l + 1, :].to_broadcast([128, 64]), dst=qgb)
        kb.op("dve", lambda eng: eng.tensor_scalar(out=qg[:, :, 0, :], in0=qg[:, :, 0, :], scalar1=0.125, scalar2=None, op0=ALU.mult),
              reads=[qgb], writes=[qgb])
        qs = sb("qs", [128, 512], F32); qsb = kb.buf("qs")
        qst = sb("qst", [128, 24], F32); qstb = kb.buf("qst")
        udT = sb("udT", [128, 4, TT], BF16); udTb = [kb.buf(f"udT{g}") for g in range(4)]

        def pass1(sname, L, l, xsrc):
            zd = zbuf[sname]
            for t in range(L // TT):
                t0 = t * TT
                for s_ in range(4):
                    kb.dma("sp", xt[:, s_, :], xsrc[t0 + s_ * 128: t0 + (s_ + 1) * 128, :], src=(xmb[sname] if l > 0 else None), dst=xtb[s_])
                norm_T(l * 3 + 0)
                for cb in range(8):
                    wh = [load_w(W[l]["mix"][cb][:, h8 * 8:(h8 + 1) * 8, :], 8, 512) for h8 in range(2)]

                    def wsel(k):
                        return wh[k // 8][0][:, k % 8, :], wh[k // 8][1]
                    if cb < 7:
                        for s_ in range(4):
                            pb_, pbb_ = next_bank()
                            for k in range(16):
                                wk_, wkb_ = wsel(k)
                                kb.op("pe", lambda eng, o=pb_[:], a=hT[:, k, s_ * 128:(s_ + 1) * 128], b=wk_, k=k:
                                      eng.matmul(o, a, b, start=(k == 0), stop=(k == 15)),
                                      reads=[hTb[k], wkb_], writes=[pbb_])
                            zi = zrr[0] % 2
                            zrr[0] += 1
                            if cb in (4, 5):
                                j = cb - 4
                                kb.op("act", lambda eng, a=pb_[:]: eng.activation(out=qs[:], in_=a, func=AF.Square),
                                      reads=[pbb_], writes=[qsb])
                                kb.op("dve", lambda eng: eng.tensor_reduce(out=qst[:, 0:8], in_=qs[:].rearrange("p (h d) -> p h d", d=64),
                                                                           axis=AX.X, op=ALU.add),
                                      reads=[qsb], writes=[qstb])
                                kb.op("dve", lambda eng: eng.tensor_scalar(out=qst[:, 8:16], in0=qst[:, 0:8], scalar1=1.0 / 64, scalar2=EPS,
                                                                           op0=ALU.mult, op1=ALU.add), reads=[qstb], writes=[qstb])
                                kb.op("act", lambda eng: eng.sqrt(out=qst[:, 16:24], in_=qst[:, 8:16]), reads=[qstb], writes=[qstb])
                                kb.op("dve", lambda eng: eng.reciprocal(out=qst[:, 0:8], in_=qst[:, 16:24]), reads=[qstb], writes=[qstb])
                                kb.op("dve", lambda eng, a=pb_[:]: eng.tensor_tensor(
                                    out=qs[:].rearrange("p (h d) -> p h d", d=64), in0=a.rearrange("p (h d) -> p h d", d=64),
                                    in1=qst[:, 0:8].unsqueeze(2).to_broadcast([128, 8, 64]), op=ALU.mult),
                                    reads=[pbb_, qstb], writes=[qsb])
                                kb.op("dve", lambda eng, o=zt[zi][:], g_=qg[:, l, j, :]: eng.tensor_tensor(out=o, in0=qs[:], in1=g_, op=ALU.mult),
                                      reads=[qsb, qgb], writes=[ztb[zi]])
                            else:
                                kb.op("act", lambda eng, o=zt[zi][:], a=pb_[:]: eng.copy(out=o, in_=a), reads=[pbb_], writes=[ztb[zi]])
                            kb.dma("pool", zd[t0 + s_ * 128: t0 + (s_ + 1) * 128, cb * 512:(cb + 1) * 512], zt[zi][:],
                                   src=ztb[zi], dst=zdb[sname])
                    else:
                        for g_ in range(4):
                            pb_, pbb_ = next_bank()
                            for k in range(16):
                                wk_, wkb_ = wsel(k)
                                kb.op("pe", lambda eng, o=pb_[:], a=wk_[:, g_ * 128:(g_ + 1) * 128], b=hT[:, k, :], k=k:
                                      eng.matmul(o, a, b, start=(k == 0), stop=(k == 15)),
                                      reads=[hTb[k], wkb_], writes=[pbb_])
                            kb.op("act", lambda eng, o=udT[:, g_, :], a=pb_[:]: eng.copy(out=o, in_=a), reads=[pbb_], writes=[udTb[g_]])
                        for s_ in range(4):
                            for gp in range(2):
                                pb_, pbb_ = next_bank()
                                for gg in range(2):
                                    g_ = gp * 2 + gg
                                    kb.op("pe", lambda eng, o=pb_[:, gg * 256:(gg + 1) * 256], a=udT[:, g_, s_ * 128:(s_ + 1) * 128]:
                                          eng.matmul(o, a, cs128[:], start=True, stop=True),
                                          reads=[udTb[g_], cs128b], writes=[pbb_])
                                zi = zrr[0] % 2
                                zrr[0] += 1
                                kb.op("act", lambda eng, o=zt[zi][:], a=pb_[:]: eng.copy(out=o, in_=a), reads=[pbb_], writes=[ztb[zi]])
                                kb.dma("pool", zd[t0 + s_ * 128: t0 + (s_ + 1) * 128, 3584 + gp * 512: 3584 + (gp + 1) * 512], zt[zi][:],
                                       src=ztb[zi], dst=zdb[sname])

        zdb = {s: kb.buf("zd_" + s) for s, _ in seqs}
        ydb = {s: kb.buf("yd_" + s) for s, _ in seqs}
        big = sb("big", [128, 44, TT], BF16)
        bigb = [kb.buf(f"big{k}") for k in range(44)]
        mT = sb("mT", [128, 16, TT], BF16); mTb = [kb.buf(f"mT{k}") for k in range(16)]
        macc = sb("macc", [128, TT], F32); maccb = kb.buf("macc")
        gt = [sb(f"gt{i}", [128, TT], F32) for i in range(2)]; gtb = [kb.buf(f"gt{i}") for i in range(2)]
        grr = [0]
        ysub = sb("ysub", [128, D], BF16); ysubb = kb.buf("ysub")
        pt = sb("ptile", [128, PLE], F32); ptb = kb.buf("ptile")
        ptb16 = sb("ptb16", [128, PLE], BF16); ptb16b = kb.buf("ptb16")
        pT = sb("pT", [128, 2, TT], BF16); pTb = kb.buf("pT")
        end_phase()

        def gtile():
            i = grr[0] % 2
            grr[0] += 1
            return gt[i], gtb[i]

        def mm_group(pb_, pbb_, pairs):
            n = len(pairs)
            for i, (a, ab, b, bb) in enumerate(pairs):
                kb.op("pe", lambda eng, o=pb_, a=a, b=b, i=i: eng.matmul(o, a, b, start=(i == 0), stop=(i == n - 1)),
                      reads=[ab, bb], writes=[pbb_])

        def pass3(sname, L, l, xsrc, xdst, xdstb):
            yd = ybuf[sname]
            for t in range(L // TT):
                t0 = t * TT
                for s_ in range(4):
                    kb.dma("sp", xt[:, s_, :], xsrc[t0 + s_ * 128: t0 + (s_ + 1) * 128, :], src=(xmb[sname] if l > 0 else None), dst=xtb[s_])
                norm_T(l * 3 + 0)
                for s_ in range(4):
                    kb.dma("sp", ysub[:], yd[t0 + s_ * 128: t0 + (s_ + 1) * 128, :], src=ydb[sname], dst=ysubb)
                    for k4 in range(4):
                        tb_, tbb_ = next_tbank()
                        for kk in range(4):
                            k = k4 * 4 + kk
                            kb.op("pe", lambda eng, o=tb_[:, kk * 128:(kk + 1) * 128], a=ysub[:, k * 128:(k + 1) * 128]:
                                  eng.transpose(out=o, in_=a, identity=ident[:]), reads=[ysubb, identb], writes=[tbb_])
                        for kk in range(4):
                            k = k4 * 4 + kk
                            kb.op("dve", lambda eng, o=big[:, k, s_ * 128:(s_ + 1) * 128], a=tb_[:, kk * 128:(kk + 1) * 128]:
                                  eng.tensor_copy(out=o, in_=a), reads=[tbb_], writes=[bigb[k]])
                sig = []
                for co in range(4):
                    wv, wvb = load_w(W[l]["glu"][co], 4, 128)
                    pb_, pbb_ = next_bank()
                    mm_group(pb_[:], pbb_, [(wv[:, ci, :], wvb, big[:, ci, :], bigb[ci]) for ci in range(4)])
                    g_, gb_ = gtile()
                    kb.op("act", lambda eng, o=g_[:], a=pb_[:]: eng.activation(out=o, in_=a, func=AF.Sigmoid), reads=[pbb_], writes=[gb_])
                    sig.append((g_, gb_))
                    if co % 2 == 1:
                        for cc in (co - 1, co):
                            g2, gb2 = sig[cc]
                            kb.op("dve", lambda eng, o=mT[:, cc, :], a=big[:, cc, :], b=g2[:]: eng.tensor_tensor(out=o, in0=a, in1=b, op=ALU.mult),
                                  reads=[bigb[cc], gb2], writes=[mTb[cc]])
                for cc in range(4):
                    kb.op("dve", lambda eng, o=big[:, cc, :], a=mT[:, cc, :]: eng.tensor_copy(out=o, in_=a), reads=[mTb[cc]], writes=[bigb[cc]])
                for ct in range(16):
                    for kbi in range(4):
                        wv, wvb = load_w(W[l]["gate"][kbi * 16 + ct], 16, 128)
                        pg_, pgb_ = next_bank()
                        mm_group(pg_[:], pgb_, [(wv[:, k, :], wvb, hT[:, k, :], hTb[k]) for k in range(16)])
                        wv2, wvb2 = load_w(W[l]["br"][kbi, ct], 4, 128)
                        pp_, ppb_ = next_bank()
                        mm_group(pp_[:], ppb_, [(wv2[:, c, :], wvb2, big[:, kbi * 4 + c, :], bigb[kbi * 4 + c]) for c in range(4)])
                        g_, gb_ = gtile()
                        kb.op("act", lambda eng, o=g_[:], a=pg_[:]: eng.activation(out=o, in_=a, func=AF.Sigmoid), reads=[pgb_], writes=[gb_])
                        if kbi == 0:
                            kb.op("dve", lambda eng, a=g_[:], b=pp_[:]: eng.tensor_tensor(out=macc[:], in0=b, in1=a, op=ALU.mult),
                                  reads=[gb_, ppb_], writes=[maccb])
                        else:
                            kb.op("dve", lambda eng, a=g_[:], b=pp_[:]: eng.tensor_tensor(out=a, in0=b, in1=a, op=ALU.mult),
                                  reads=[gb_, ppb_], writes=[gb_])
                            if kbi < 3:
                                kb.op("dve", lambda eng, a=g_[:]: eng.tensor_tensor(out=macc[:], in0=macc[:], in1=a, op=ALU.add),
                                      reads=[gb_, maccb], writes=[maccb])
                            else:
                                kb.op("dve", lambda eng, a=g_[:], o=mT[:, ct, :]: eng.tensor_tensor(out=o, in0=macc[:], in1=a, op=ALU.add),
                                      reads=[gb_, maccb], writes=[mTb[ct]])

                def tok_major(wkey, nk, srcT, srcTb, fin, kpart):
                    for cb in range(4):
                        banks = [next_bank() for _ in range(4)]
                        for k0 in range(0, nk, kpart):
                            k1 = min(nk, k0 + kpart)
                            wv, wvb = load_w(W[l][wkey][cb][:, k0:k1, :], k1 - k0, 512)
                            for s_ in range(4):
                                pb_, pbb_ = banks[s_]
                                for k in range(k0, k1):
                                    kb.op("pe", lambda eng, o=pb_[:], a=srcT[:, k, s_ * 128:(s_ + 1) * 128], b=wv[:, k - k0, :], k=k:
                                          eng.matmul(o, a, b, start=(k == 0), stop=(k == nk - 1)),
                                          reads=[srcTb[k], wvb], writes=[pbb_])
                        for s_ in range(4):
                            fin(cb, s_, banks[s_][0], banks[s_][1])

                def add_resid(cb, s_, pb_, pbb_):
                    kb.op("dve", lambda eng, x_=xt[:, s_, cb * 512:(cb + 1) * 512], a=pb_[:]: eng.tensor_tensor(out=x_, in0=a, in1=x_, op=ALU.add),
                          reads=[pbb_, xtb[s_]], writes=[xtb[s_]])

                tok_major("o", 16, mT, mTb, add_resid, 8)
                norm_T(l * 3 + 1)
                for ft_ in range(44):
                    wva, wvab = load_w(W[l]["f1"][ft_], 16, 128)
                    pa_, pab_ = next_bank()
                    mm_group(pa_[:], pab_, [(wva[:, k, :], wvab, hT[:, k, :], hTb[k]) for k in range(16)])
                    wvb_, wvbb_ = load_w(W[l]["f1"][44 + ft_], 16, 128)
                    pb2_, pb2b_ = next_bank()
                    mm_group(pb2_[:], pb2b_, [(wvb_[:, k, :], wvbb_, hT[:, k, :], hTb[k]) for k in range(16)])
                    g_, gb_ = gtile()
                    kb.op("act", lambda eng, o=g_[:], a=pa_[:]: eng.activation(out=o, in_=a, func=AF.Silu), reads=[pab_], writes=[gb_])
                    kb.op("dve", lambda eng, o=big[:, ft_, :], a=g_[:], b=pb2_[:]: eng.tensor_tensor(out=o, in0=b, in1=a, op=ALU.mult),
                          reads=[gb_, pb2b_], writes=[bigb[ft_]])
                tok_major("f2", 44, big, bigb, add_resid, 11)
                norm_T(l * 3 + 2)
                for s_ in range(4):
                    kb.dma("sp", pt[:], pin[sname][l, t0 + s_ * 128:t0 + (s_ + 1) * 128, :], dst=ptb)
                    kb.op("dve", lambda eng: eng.tensor_copy(out=ptb16[:], in_=pt[:]), reads=[ptb], writes=[ptb16b])
                    tb_, tbb_ = next_tbank()
                    for kk in range(2):
                        kb.op("pe", lambda eng, o=tb_[:, kk * 128:(kk + 1) * 128], a=ptb16[:, kk * 128:(kk + 1) * 128]:
                              eng.transpose(out=o, in_=a, identity=ident[:]), reads=[ptb16b, identb], writes=[tbb_])
                    kb.op("dve", lambda eng, o=pT[:, :, s_ * 128:(s_ + 1) * 128], a=tb_[:, 0:256].rearrange("p (k c) -> p k c", c=128):
                          eng.tensor_copy(out=o, in_=a), reads=[tbb_], writes=[pTb])
                for cb in range(4):
                    wh = [load_w(W[l]["pg"][cb][:, h8 * 8:(h8 + 1) * 8, :], 8, 512) for h8 in range(2)]
                    wv2, wvb2 = load_w(W[l]["pp"][cb], 2, 512)
                    for s_ in range(4):
                        pg_, pgb_ = next_bank()
                        mm_group(pg_[:], pgb_, [(hT[:, k, s_ * 128:(s_ + 1) * 128], hTb[k], wh[k // 8][0][:, k % 8, :], wh[k // 8][1]) for k in range(16)])
                        pp_, ppb_ = next_bank()
                        mm_group(pp_[:], ppb_, [(pT[:, k, s_ * 128:(s_ + 1) * 128], pTb, wv2[:, k, :], wvb2) for k in range(2)])
                        g_, gb_ = gtile()
                        kb.op("act", lambda eng, o=g_[:], a=pg_[:]: eng.activation(out=o, in_=a, func=AF.Sigmoid), reads=[pgb_], writes=[gb_])
                        kb.op("dve", lambda eng, a=g_[:], b=pp_[:]: eng.tensor_tensor(out=a, in0=b, in1=a, op=ALU.mult),
                              reads=[gb_, ppb_], writes=[gb_])
                        kb.op("dve", lambda eng, x_=xt[:, s_, cb * 512:(cb + 1) * 512], a=g_[:]: eng.tensor_tensor(out=x_, in0=a, in1=x_, op=ALU.add),
                              reads=[gb_, xtb[s_]], writes=[xtb[s_]])
                for s_ in range(4):
                    kb.dma("pool", xdst[t0 + s_ * 128: t0 + (s_ + 1) * 128, :], xt[:, s_, :], src=xtb[s_], dst=xdstb)

        def mixers(sname, L, l):
            MIXERS(kb, nc, sb, sname, L, l, zbuf[sname], zdb[sname], ybuf[sname], ydb[sname], locals_=dict(
                next_bank=next_bank, next_tbank=next_tbank, ident=ident, identb=identb, new_phase=new_phase,
                end_phase=end_phase, push_scope=push_scope, pop_scope=pop_scope, P=PRM, CST=CST, ft=ft[sname], pbank=pbank, pbuf=pbuf, tbank=tbank, tbuf=tbuf))

        for sname, L in seqs:
            for l in range(NLAYER):
                xsrc = xin[sname] if l == 0 else xmid[sname]
                last = (l == NLAYER - 1)
                xdst = yout[sname] if last else xmid[sname]
                xdstb = outb if last else xmb[sname]
                kb.barrier()
                pass1(sname, L, l, xsrc)
                kb.barrier()
                mixers(sname, L, l)
                kb.barrier()
                if DEBUG_Y:
                    new_phase("dbg")
                    dbg = sb("dbg", [128, D], BF16); dbgb = kb.buf("dbg")
                    dbgf = sb("dbgf", [128, D], F32); dbgfb = kb.buf("dbgf")
                    for t in range(L // 128):
                        kb.dma("sp", dbg[:], ybuf[sname][t * 128:(t + 1) * 128, :], src=ydb[sname], dst=dbgb)
                        kb.op("dve", lambda eng: eng.tensor_copy(out=dbgf[:], in_=dbg[:]), reads=[dbgb], writes=[dbgfb])
                        kb.dma("pool", yout[sname][t * 128:(t + 1) * 128, :], dbgf[:], src=dbgfb, dst=outb)
                    end_phase()
                    continue
                pass3(sname, L, l, xsrc, xdst, xdstb)
        kb.finish_wait("pool", [outb, wscr])
        with nc.allow_non_contiguous_dma(reason="small strided tables"), nc.Block() as block:
            kb.replay(block)
    return nc


def _bf(a):
    return np.ascontiguousarray(np.asarray(a, np.float32).astype(ml_dtypes.bfloat16))


def make_in_map(inp, xp, pp, xs, ps_, LP, LS):
    m = {"x_p": np.ascontiguousarray(xp), "p_p": np.ascontiguousarray(pp), "x_s": np.ascontiguousarray(xs),
         "p_s": np.ascontiguousarray(ps_)}
    for k in ("g_mix", "g_ffn", "g_ple", "w_in", "s5_lam_re", "s5_lam_im", "s5_log_dt", "s5_b_re", "s5_b_im", "s5_c_re",
              "s5_c_im", "s5_d", "w_glu", "conv_w", "q_gain", "k_gain", "rel_bias", "w_br", "w_o", "w_ffn_in", "w_ffn_out",
              "w_ple_gate", "w_ple_proj"):
        m[k] = np.ascontiguousarray(np.asarray(inp[k], np.float32))
    m["c_ident"] = _bf(np.eye(128))
    m["c_J"] = _bf(np.eye(128)[::-1])
    m["c_iota1"] = np.ascontiguousarray(np.tile(np.arange(1, 513, dtype=np.float32)[None, :], (128, 1)))
    pidx = np.arange(128)
    m["c_maskh"] = np.ascontiguousarray(np.stack([(((pidx // 32) % 2) == sel) & (((pidx // 16) % 2) == g2)
                                                  for sel in range(2) for g2 in range(2)], axis=1).astype(np.float32))
    m["c_maskp"] = np.ascontiguousarray(np.stack([(pidx // 64) == 0, (pidx // 64) == 1], axis=1).astype(np.float32))
    m["c_maskpn"] = -m["c_maskp"]
    m["c_pi"] = np.full((128, 1), math.pi, np.float32)
    qc = np.arange(64); kc = np.arange(64)
    cs_ = np.clip(qc - 8, 0, 48)
    valid = (kc[:, None] >= cs_[None, :]) & (kc[:, None] < cs_[None, :] + 16)
    midx = kc[:, None] - qc[None, :] + 15
    oh = np.zeros((2, 31, 64, 2, 64), np.float32)
    for a in range(2):
        for k_ in range(64):
            for q_ in range(64):
                if valid[k_, q_]:
                    oh[a, midx[k_, q_], q_, a, k_] = 1.0
    m["c_oh2"] = _bf(oh.reshape(62, 64, 128))
    mk = np.where(valid, 0.0, -30000.0).astype(np.float32)
    m["c_mask2"] = np.ascontiguousarray(np.concatenate([mk, mk], axis=0))
    n = np.arange(128)
    ang = 2 * np.pi * np.outer(n, n) / 128.0
    C1, S1 = np.cos(ang), np.sin(ang)
    m["c_cs128"] = _bf(np.concatenate([C1, S1], axis=1))
    m["c_r1"] = _bf(np.concatenate([C1, S1], axis=1))
    m["c_r2"] = _bf(np.concatenate([-S1, C1], axis=1))
    for s, L in (("p", LP), ("s", LS)):
        c = _consts(L)
        m["c_tc_" + s] = np.ascontiguousarray(c["Tc"].astype(np.float32))
        m["c_ts_" + s] = np.ascontiguousarray(c["Ts"].astype(np.float32))
        m["c_c2_" + s] = _bf(c["C2"])
        m["c_s2n_" + s] = _bf(c["S2n"])
    return m


_PROG = {}


def kernel(**inputs):
    LP, LS = 16384, 4096
    xp = np.asarray(inputs["x_prompt"], np.float32)
    xs = np.asarray(inputs["x_sample"], np.float32)
    pp = np.asarray(inputs["p_prompt"], np.float32)
    ps_ = np.asarray(inputs["p_sample"], np.float32)
    if "nc" not in _PROG:
        _PROG["nc"] = build_program(LP, LS)
    nc = _PROG["nc"]
    zxp = np.zeros((LP, D), np.float32); zpp = np.zeros((DEPTH, LP, PLE), np.float32)
    zxs = np.zeros((LS, D), np.float32); zps = np.zeros((DEPTH, LS, PLE), np.float32)
    in_maps = []
    for c in range(8):
        a = (xp[c], pp[:, c]) if c < 2 else (zxp, zpp)
        b = (xs[c], ps_[:, c]) if c < 4 else (zxs, zps)
        in_maps.append(make_in_map(inputs, a[0], a[1], b[0], b[1], LP, LS))
    res = run_bass_kernel_spmd(nc, in_maps, core_ids=list(range(8)))
    yp = np.stack([np.asarray(res.results[c]["y_p"], np.float32) for c in range(2)], axis=0)
    ys = np.stack([np.asarray(res.results[c]["y_s"], np.float32) for c in range(4)], axis=0)
    return (yp, ys)
```
